# Optimizing a Trainium2 kernel written in Bass

```python
import jax, jax.numpy as jnp
from jax import lax
import numpy as np

D_MODEL = 1024
BATCH = 8
SEQ = 8192
DEPTH = 2
DEC_BATCH = 32
DEC_SEQ = 64
PAST_LEN = 4096

CHUNK = 64
MLP_CHUNK = 128
HEAD_DIM = 64
D_A = 512
N_HEADS_A = D_A // HEAD_DIM
D_B = 256
POOL_WINDOWS = (2, 4, 8, 16)
N_POOL_GROUPS = len(POOL_WINDOWS)
POOL_GROUP_DIM = D_B // N_POOL_GROUPS
POOL_STATE = max(POOL_WINDOWS) - 1
D_C = 256
N_HEADS_C = D_C // HEAD_DIM
CONV_W = 3
D_MIX = D_A + D_B + D_C
D_IN = 2 * D_A + D_B + 3 * D_C
D_FF = -(-8 * D_MODEL // (3 * 256)) * 256
ALPHA = (2 * DEPTH) ** 0.25
BETA = (8 * DEPTH) ** -0.25
LN_EPS = 1e-5

kernel_name = 'hybrid_stream_gmlp_pool_shortconv_step'


def layer_norm(x, g, b):
    xf = x.astype(jnp.float32)
    mu = jnp.mean(xf, axis=-1, keepdims=True)
    var = jnp.mean(jnp.square(xf - mu), axis=-1, keepdims=True)
    return ((xf - mu) * lax.rsqrt(var + LN_EPS) * g.astype(jnp.float32) + b.astype(jnp.float32)).astype(x.dtype)


def chunk_mlp(u, v, w_spatial, b_spatial):
    bsz, s, _ = v.shape
    pad = (-s) % MLP_CHUNK
    vp = jnp.pad(v, ((0, 0), (0, pad), (0, 0)))
    n = (s + pad) // MLP_CHUNK
    vb = vp.reshape(bsz, n, MLP_CHUNK, N_HEADS_A, HEAD_DIM)
    blk = jnp.arange(MLP_CHUNK) // CHUNK
    mask = blk[None, :] <= blk[:, None]
    ws = jnp.where(mask[None], w_spatial, jnp.zeros((), w_spatial.dtype))
    z = jnp.einsum('hij,bnjhd->bnihd', ws, vb) + b_spatial.T[None, None, :, :, None]
    z = z.reshape(bsz, n * MLP_CHUNK, D_A)[:, :s]
    return u * z


def pool_mixer(xb, pool_prefix, pos0, w_pool, pool_scale):
    bsz, s, _ = xb.shape
    padded = jnp.concatenate([pool_prefix.astype(xb.dtype), xb], axis=1)
    cs = jnp.cumsum(padded.astype(jnp.float32), axis=1)
    cs = jnp.pad(cs, ((0, 0), (1, 0), (0, 0)))
    pos = pos0 + jnp.arange(s)
    end = POOL_STATE + 1
    groups = []
    for g, w in enumerate(POOL_WINDOWS):
        c0 = g * POOL_GROUP_DIM
        c1 = c0 + POOL_GROUP_DIM
        win = cs[:, end:end + s, c0:c1] - cs[:, end - w:end - w + s, c0:c1]
        cnt = jnp.minimum(pos + 1, w).astype(jnp.float32)[None, :, None]
        groups.append(win / cnt)
    pooled = jnp.stack(groups, axis=2)
    xg = xb.reshape(bsz, s, N_POOL_GROUPS, POOL_GROUP_DIM).astype(jnp.float32)
    diff = (pooled - xg).astype(xb.dtype)
    out = jnp.einsum('bsgc,gce->bsge', diff, w_pool).reshape(bsz, s, D_B) * pool_scale
    return out, padded[:, -POOL_STATE:]


def short_conv(gb, gc, h, conv_prefix, conv_w):
    s = h.shape[1]
    z = gc * h
    padded = jnp.concatenate([conv_prefix.astype(z.dtype), z], axis=1)
    y = conv_w[0] * padded[:, 0:s]
    for k in range(1, CONV_W):
        y = y + conv_w[k] * padded[:, k:k + s]
    return gb * y, padded[:, -(CONV_W - 1):]


def trunk_layer(x, pool_prefix, conv_prefix, pos0, ln_mix_g, ln_mix_b, w_in, v_norm_g, v_norm_b,
                w_spatial, b_spatial, w_pool, pool_scale, conv_w, w_out,
                ln_ffn_g, ln_ffn_b, w_gate, w_up, w_down):
    proj = jnp.einsum('bsd,de->bse', x, w_in)
    cuts = [D_A, 2 * D_A, 2 * D_A + D_B, 2 * D_A + D_B + D_C, 2 * D_A + D_B + 2 * D_C]
    u, v, xb, gb, gc, h = jnp.split(proj, cuts, axis=-1)
    v = layer_norm(v, v_norm_g, v_norm_b)
    out_a = chunk_mlp(u, v, w_spatial, b_spatial)
    out_b, pool_state = pool_mixer(xb, pool_prefix, pos0, w_pool, pool_scale)
    out_c, conv_state = short_conv(gb, gc, h, conv_prefix, conv_w)
    mix = jnp.einsum('bse,ed->bsd', jnp.concatenate([out_a, out_b, out_c], axis=-1), w_out)
    x = layer_norm(ALPHA * x + mix, ln_mix_g, ln_mix_b)
    hid = jax.nn.silu(jnp.einsum('bsd,df->bsf', x, w_gate)) * jnp.einsum('bsd,df->bsf', x, w_up)
    ffn = jnp.einsum('bsf,fd->bsd', hid, w_down)
    x = layer_norm(ALPHA * x + ffn, ln_ffn_g, ln_ffn_b)
    return x, pool_state, conv_state, v


def setup_inputs(seed: int = 0) -> dict:
    key = jax.random.key(seed)
    ks = jax.random.split(key, 24)
    f32 = jnp.float32
    nrm = lambda k, shp, sc: jax.random.normal(k, shp, f32) * sc
    return {
        'x_prompt': nrm(ks[0], (BATCH, SEQ, D_MODEL), 1.0),
        'x_sample': nrm(ks[1], (DEC_BATCH, DEC_SEQ, D_MODEL), 1.0),
        'state_pool': nrm(ks[2], (DEPTH, DEC_BATCH, POOL_STATE, D_B), 1.0),
        'state_conv': nrm(ks[3], (DEPTH, DEC_BATCH, CONV_W - 1, D_C), 1.0),
        'ln_mix_g': 1.0 + nrm(ks[4], (DEPTH, D_MODEL), 0.02),
        'ln_mix_b': nrm(ks[5], (DEPTH, D_MODEL), 0.02),
        'w_in': nrm(ks[6], (DEPTH, D_MODEL, D_IN), D_MODEL ** -0.5),
        'v_norm_g': 1.0 + nrm(ks[7], (DEPTH, D_A), 0.02),
        'v_norm_b': nrm(ks[8], (DEPTH, D_A), 0.02),
        'w_spatial': nrm(ks[9], (DEPTH, N_HEADS_A, MLP_CHUNK, MLP_CHUNK), MLP_CHUNK ** -0.5),
        'b_spatial': 1.0 + nrm(ks[10], (DEPTH, N_HEADS_A, MLP_CHUNK), 0.02),
        'w_pool': nrm(ks[11], (DEPTH, N_POOL_GROUPS, POOL_GROUP_DIM, POOL_GROUP_DIM), POOL_GROUP_DIM ** -0.5),
        'pool_scale': 1.0 + nrm(ks[12], (DEPTH, D_B), 0.1),
        'conv_w': nrm(ks[13], (DEPTH, CONV_W, D_C), CONV_W ** -0.5),
        'w_out': nrm(ks[14], (DEPTH, D_MIX, D_MODEL), BETA * D_MIX ** -0.5),
        'ln_ffn_g': 1.0 + nrm(ks[15], (DEPTH, D_MODEL), 0.02),
        'ln_ffn_b': nrm(ks[16], (DEPTH, D_MODEL), 0.02),
        'w_gate': nrm(ks[17], (DEPTH, D_MODEL, D_FF), D_MODEL ** -0.5),
        'w_up': nrm(ks[18], (DEPTH, D_MODEL, D_FF), D_MODEL ** -0.5),
        'w_down': nrm(ks[19], (DEPTH, D_FF, D_MODEL), BETA * D_FF ** -0.5),
    }


def reference(x_prompt, x_sample, state_pool, state_conv, ln_mix_g, ln_mix_b, w_in, v_norm_g, v_norm_b,
              w_spatial, b_spatial, w_pool, pool_scale, conv_w, w_out,
              ln_ffn_g, ln_ffn_b, w_gate, w_up, w_down):
    bp = x_prompt.shape[0]
    zero_pool = jnp.zeros((bp, POOL_STATE, D_B), x_prompt.dtype)
    zero_conv = jnp.zeros((bp, CONV_W - 1, D_C), x_prompt.dtype)
    yp, ys = x_prompt, x_sample
    pool_p, pool_s, conv_p, conv_s, v_s = [], [], [], [], []
    for l in range(DEPTH):
        params = (ln_mix_g[l], ln_mix_b[l], w_in[l], v_norm_g[l], v_norm_b[l],
                  w_spatial[l], b_spatial[l], w_pool[l], pool_scale[l], conv_w[l], w_out[l],
                  ln_ffn_g[l], ln_ffn_b[l], w_gate[l], w_up[l], w_down[l])
        yp, sp_pool, sp_conv, _ = trunk_layer(yp, zero_pool, zero_conv, 0, *params)
        ys, ss_pool, ss_conv, ss_v = trunk_layer(ys, state_pool[l], state_conv[l], PAST_LEN, *params)
        pool_p.append(sp_pool)
        conv_p.append(sp_conv)
        pool_s.append(ss_pool)
        conv_s.append(ss_conv)
        v_s.append(ss_v)
    new_pool_prompt = jnp.stack(pool_p, axis=0)
    new_pool_sample = jnp.stack(pool_s, axis=0)
    new_conv_prompt = jnp.stack(conv_p, axis=0)
    new_conv_sample = jnp.stack(conv_s, axis=0)
    new_chunk_v_sample = jnp.stack(v_s, axis=0)
    return (yp, ys, new_pool_prompt, new_pool_sample, new_conv_prompt, new_conv_sample, new_chunk_v_sample)
```

```python
import contextlib
import numpy as np
import concourse.bass as bass
import concourse.mybir as mybir
from concourse.bass_utils import run_bass_kernel_spmd

F32 = mybir.dt.float32
BF16 = mybir.dt.bfloat16
AF = mybir.ActivationFunctionType
ALU = mybir.AluOpType

D = 1024
DFF = 2816
NFC = 22
ALPHA = float(4 ** 0.25)
EPS = 1e-5
NSLOT = 15
NUNIT = 47
U_V0, U_V1, U_XB, U_GC, U_H, U_U01, U_U23, U_GB, U_OUT0, U_GU0, U_DN0 = 0, 1, 2, 3, 4, 5, 6, 7, 8, 12, 36
N_CORES = 8
NPT_FULL = 16


class _Op:
    __slots__ = ("eng", "fn", "deps", "is_dma", "chan", "sig", "idx")

    def __init__(self, eng, fn, is_dma, chan):
        self.eng = eng
        self.fn = fn
        self.deps = []
        self.is_dma = is_dma
        self.chan = chan
        self.sig = 0
        self.idx = 0


class Sched:
    ENGS = ("pe", "act", "dve", "pool", "sp")

    def __init__(self, nc):
        self.nc = nc
        self.ops = {e: [] for e in self.ENGS}
        self.res = {}
        self.chan_count = {}
        self.n = 0

    def add(self, eng, fn, reads=(), writes=(), dma_chan=None):
        op = _Op(eng, fn, dma_chan is not None, dma_chan)
        op.idx = self.n
        self.n += 1
        deps = {}
        res = self.res

        def dep(o, raw):
            if (not o.is_dma) and (not op.is_dma) and o.eng == op.eng and op.eng == "pe":
                return
            deps[o.idx] = o

        for k in reads:
            st = res.get(k)
            if st is None:
                st = res[k] = [{}, {}]
            for o in st[0].values():
                dep(o, True)
        for k in writes:
            st = res.get(k)
            if st is None:
                st = res[k] = [{}, {}]
            for o in st[1].values():
                dep(o, False)
            for o in st[0].values():
                dep(o, False)
        tok = ("dma", dma_chan) if op.is_dma else eng
        for k in reads:
            res[k][1][tok] = op
        for k in writes:
            st = res[k]
            if st[1]:
                st[0] = {tok: op}
                st[1] = {}
            else:
                st[0][tok] = op
        op.deps = list(deps.values())
        if op.is_dma:
            c = self.chan_count.get(dma_chan, 0) + 1
            self.chan_count[dma_chan] = c
            op.sig = c
        self.ops[eng].append(op)
        return op

    def emit(self, final_wait_eng="sp"):
        nc = self.nc
        for e in self.ENGS:
            for op in self.ops[e]:
                for d in op.deps:
                    if not d.is_dma and d.sig == 0:
                        d.sig = -1
        for e in self.ENGS:
            c = 0
            for op in self.ops[e]:
                if not op.is_dma and op.sig == -1:
                    c += 1
                    op.sig = c
        chans = sorted(self.chan_count.keys(), key=str)
        with contextlib.ExitStack() as st:
            esem = {e: st.enter_context(nc.semaphore("s_" + e)) for e in self.ENGS}
            csem = {}
            for i, c in enumerate(chans):
                csem[c] = st.enter_context(nc.semaphore("c%d" % i))
            block = st.enter_context(nc.Block())
            engobj = {"pe": block.tensor, "act": block.scalar, "dve": block.vector,
                      "pool": block.gpsimd, "sp": block.sync}

            def make(e):
                def body(eng):
                    waited = {}
                    for op in self.ops[e]:
                        need = {}
                        for d in op.deps:
                            if d.is_dma:
                                key = ("c", d.chan)
                                val = 16 * d.sig
                                sem = csem[d.chan]
                            else:
                                key = ("e", d.eng)
                                val = d.sig
                                sem = esem[d.eng]
                            if need.get(key, (None, 0))[1] < val:
                                need[key] = (sem, val)
                        todo = []
                        for key, (sem, val) in need.items():
                            if waited.get(key, 0) >= val:
                                continue
                            waited[key] = val
                            todo.append((sem, val))
                        for sem, val in todo[:-1]:
                            eng.wait_ge(sem, val)
                        inst = op.fn(eng)
                        if todo:
                            inst._wait_ge(todo[-1][0], todo[-1][1])
                        if op.is_dma:
                            inst.then_inc(csem[op.chan], 16)
                        elif op.sig > 0:
                            inst.then_inc(esem[op.eng], 1)
                    if e == final_wait_eng:
                        for c in chans:
                            eng.wait_ge(csem[c], 16 * self.chan_count[c])
                return body

            for e in self.ENGS:
                if self.ops[e] or e == final_wait_eng:
                    engobj[e](make(e))


def build(npt, with_sample=True):
    nc = bass.Bass("TRN2", target_bir_lowering=False)
    NP = npt * 512

    def din(name, shape):
        return nc.dram_tensor(name, shape, F32, kind="ExternalInput").ap()

    def dout(name, shape):
        return nc.dram_tensor(name, shape, F32, kind="ExternalOutput").ap()

    xp = din("xp", [NP, D])
    xs = din("xs", [256, D])
    spool = din("spool", [2, 60, 256])
    sconv = din("sconv", [2, 8, 256])
    ln_mix_g = din("ln_mix_g", [2, D]); ln_mix_b = din("ln_mix_b", [2, D])
    ln_ffn_g = din("ln_ffn_g", [2, D]); ln_ffn_b = din("ln_ffn_b", [2, D])
    v_norm_g = din("v_norm_g", [2, 512]); v_norm_b = din("v_norm_b", [2, 512])
    w_in = din("w_in", [2, D, 2048])
    w_spatial = din("w_spatial", [2, 8, 128, 128])
    b_spatial = din("b_spatial", [2, 8, 128])
    w_pool = din("w_pool", [2, 4, 64, 64])
    pool_scale = din("pool_scale", [2, 256])
    conv_w = din("conv_w", [2, 3, 256])
    w_out = din("w_out", [2, D, D])
    w_gate = din("w_gate", [2, D, DFF]); w_up = din("w_up", [2, D, DFF])
    w_down = din("w_down", [2, DFF, D])
    ident = din("ident", [128, 128])
    cst = din("cst", [128, 34])

    yp = dout("yp", [NP, D])
    ys = dout("ys", [256, D])
    o_pool_p = dout("o_pool_p", [2, 15, 256])
    o_pool_s = dout("o_pool_s", [2, 60, 256])
    o_conv_p = dout("o_conv_p", [2, 2, 256])
    o_conv_s = dout("o_conv_s", [2, 8, 256])
    o_v = dout("o_v", [2, 256, 512])

    wbf = nc.dram_tensor("wbf", [2, NUNIT, 128, 2048], BF16).ap()

    sb = nc.alloc_sbuf_tensor
    ring = sb("ring", [128, NSLOT, 2048], BF16)
    xres = sb("xres", [128, 4, D], F32)
    xbf = sb("xbf", [128, 4, D], BF16)
    xT = sb("xT", [128, 8, 512], BF16)
    vscr = sb("vscr", [128, 2, 512], F32)
    vn = sb("vn", [128, 4, 512], BF16)
    xbuf = sb("xbuf", [128, 2, 528], F32)
    pscr = sb("pscr", [128, 2, 528], F32)
    zbuf = sb("zbuf", [128, 2, 516], F32)
    ybuf = sb("ybuf", [128, 2, 512], F32)
    diff = sb("diff", [128, 2, 512], BF16)
    cat = sb("cat", [128, 8, 512], BF16)
    hid = sb("hid", [128, NFC, 512], BF16)
    sscr = sb("sscr", [128, 2, 512], F32)
    lscr = sb("lscr", [128, 2, D], F32)
    mixg = sb("mixg", [128, D], F32); mixb = sb("mixb", [128, D], F32)
    ffng = sb("ffng", [128, D], F32); ffnb = sb("ffnb", [128, D], F32)
    vng = sb("vng", [128, 512], F32); vnb = sb("vnb", [128, 512], F32)
    wsT = sb("wsT", [128, 32, 128], BF16)
    wpbd = sb("wpbd", [128, 4, 128], BF16)
    bmat = sb("bmat", [128, 16, 128], F32)
    ctmp = sb("ctmp", [128, 2, 512], F32)
    idf = sb("idf", [128, 128], F32)
    idb = sb("idb", [128, 128], BF16)
    cst_sb = sb("cst_sb", [128, 34], F32)
    pscale = sb("pscale", [128, 2, 2], F32)
    cw = sb("cw", [128, 2, 2, 3], F32)
    epst = sb("epst", [128, 1], F32)
    ptail = sb("ptail", [128, 2, 2, 15], F32)
    ctail = sb("ctail", [128, 2, 2, 2], F32)
    stin = sb("stin", [64, 256], F32)
    stin2 = sb("stin2", [8, 256], F32)
    sttm = sb("sttm", [128, 1, 256], F32)
    sttm2 = sb("sttm2", [128, 1, 256], F32)
    vst = sb("vst", [128, 4, 6], F32); vmv = sb("vmv", [128, 4, 2], F32); vrs = sb("vrs", [128, 4], F32)
    lsum = sb("lsum", [128, 4], F32); lmean = sb("lmean", [128, 4], F32)
    lnb = sb("lnb", [128, 4], F32); lsq = sb("lsq", [128, 4], F32); lrs = sb("lrs", [128, 4], F32)
    ps = nc.alloc_psum_tensor("ps", [128, 8, 512], F32)

    S = Sched(nc)
    st = {"pb": 0, "cv": 0, "ld": 0, "io": 0}

    def alloc1():
        b = st["pb"] % 8
        st["pb"] += 1
        return b

    def alloc2():
        if st["pb"] % 2:
            st["pb"] += 1
        b = st["pb"] % 8
        st["pb"] += 2
        return b

    def PS(b):
        return ("ps", b)

    def psb(b):
        return ps[:, b, :].bitcast(BF16).rearrange("p (k n) -> p k n", k=8)

    UL = [("in", 512, 0, 4), ("in", 512, 4, 4)]
    for c0 in (1024, 1536, 1792, 0, 256, 1280):
        UL.append(("inF", c0))
    for j in range(4):
        UL.append(("out", j))
    for g in range(6):
        for j in range(4):
            UL.append(("gu", g, j))
    for j in range(11):
        UL.append(("down", j))
    assert len(UL) == NUNIT

    def unit_keys(l, ui):
        if UL[ui][0] == "gu":
            return [("wbf", l, ui, 0), ("wbf", l, ui, 1)]
        return [("wbf", l, ui, 0)]

    def emit_conv(l, ui):
        k = UL[ui]
        dst = wbf[l, ui]
        jobs = []
        if k[0] == "in":
            _, c0, kc0, nk = k
            src = w_in[l, kc0 * 128:(kc0 + nk) * 128, c0:c0 + 512].rearrange("(k p) n -> p k n", p=128)
            jobs.append((dst.rearrange("p (k n) -> p k n", k=nk), src, 0))
        elif k[0] == "inF":
            c0 = k[1]
            src = w_in[l, :, c0:c0 + 256].rearrange("(k p) n -> p k n", p=128)
            jobs.append((dst.rearrange("p (k n) -> p k n", k=8), src, 0))
        elif k[0] == "out":
            j = k[1]
            src = w_out[l, j * 256:(j + 1) * 256, :].rearrange("(k p) n -> p k n", p=128)
            jobs.append((dst.rearrange("p (k n) -> p k n", k=2), src, 0))
        elif k[0] == "gu":
            _, g, j = k
            nf = 4 if g < 5 else 2
            f0 = 4 * g
            for wi, wsrc in enumerate((w_gate, w_up)):
                src = wsrc[l, j * 256:(j + 1) * 256, f0 * 128:(f0 + nf) * 128].rearrange("(k p) n -> p k n", p=128)
                d = dst[:, wi * 2 * nf * 128:(wi + 1) * 2 * nf * 128].rearrange("p (k n) -> p k n", k=2)
                jobs.append((d, src, wi))
        else:
            j = k[1]
            src = w_down[l, j * 256:(j + 1) * 256, :].rearrange("(k p) n -> p k n", p=128)
            jobs.append((dst.rearrange("p (k n) -> p k n", k=2), src, 0))
        for d, s_, wi in jobs:
            ch = st["cv"] % 8
            st["cv"] += 1
            S.add("pool", lambda e, d=d, s_=s_: e.dma_start(out=d, in_=s_),
                  writes=[("wbf", l, ui, wi), ("cvch", ch)], dma_chan=("cv", ch))

    tiles = [("p", t) for t in range(npt)] + ([("s", 0)] if with_sample else [])
    seq = [(ti, l, ui) for ti in range(len(tiles)) for l in range(2) for ui in range(NUNIT)]

    def emit_load(n):
        ti, l, ui = seq[n]
        s_ = n % NSLOT
        ne = 1024 if (UL[ui][0] == "gu" and UL[ui][1] == 5) else 2048
        S.add("sp", lambda e: e.dma_start(out=ring[:, s_, 0:ne], in_=wbf[l, ui][:, 0:ne]),
              reads=unit_keys(l, ui), writes=[("ring", s_)], dma_chan=("w", s_))

    def slot(n):
        assert n < st["ld"], (n, st["ld"])
        return n % NSLOT

    def done(n):
        m = n + NSLOT
        assert m == st["ld"], (m, st["ld"])
        if m < len(seq):
            emit_load(m)
        st["ld"] += 1

    def tile_src(ti):
        kind, t = tiles[ti]
        if kind == "p":
            return xp[t * 512:(t + 1) * 512, :], yp[t * 512:(t + 1) * 512, :], 4
        return xs, ys, 2

    def load_xres(ti):
        src, _, nch = tile_src(ti)
        for c in range(nch):
            S.add("pool", lambda e, c=c: e.dma_start(out=xres[:, c, :], in_=src[c * 128:(c + 1) * 128, :]),
                  writes=[("xres", c)], dma_chan=("xl", c))

    def load_xbf(ti):
        src, _, nch = tile_src(ti)
        for c in range(nch):
            S.add("pool", lambda e, c=c: e.dma_start(out=xbf[:, c, :], in_=src[c * 128:(c + 1) * 128, :]),
                  writes=[("xbf", c)], dma_chan=("xb", c))

    load_xbf(0)
    load_xres(0)
    for l in range(2):
        for ui in range(NUNIT):
            emit_conv(l, ui)
    for n in range(5):
        emit_load(n)

    def pdma(fn, reads=(), writes=()):
        ch = st["io"]
        st["io"] += 1
        return S.add("sp", fn, reads=list(reads), writes=list(writes), dma_chan=("pr", ch))

    pdma(lambda e: e.dma_start(out=idf[:], in_=ident), writes=["idf"])
    pdma(lambda e: e.dma_start(out=cst_sb[:], in_=cst), writes=["cst"])
    S.add("act", lambda e: e.copy(out=idb[:], in_=idf[:]), reads=["idf"], writes=["idb"])
    S.add("dve", lambda e: e.memset(epst[:], EPS), writes=["eps"])
    S.add("dve", lambda e: e.memset(ptail[:], 0.0), writes=["ptail0", "ptail1"])
    S.add("dve", lambda e: e.memset(ctail[:], 0.0), writes=["ctail0", "ctail1"])
    for l in range(2):
        pdma(lambda e, l=l: e.dma_start(out=pscale[:, l, :], in_=pool_scale[l].rearrange("(c p) -> p c", p=128)), writes=[("pscale", l)])
        for k in range(3):
            pdma(lambda e, l=l, k=k: e.dma_start(out=cw[:, l, :, k], in_=conv_w[l, k].rearrange("(c p) -> p c", p=128)), writes=[("cw", l, k)])

    stA = lscr[:, :, :].rearrange("p a (h j) -> p (a h) j", j=128)
    stB = hid[:, 0:8, :].rearrange("p a b -> p (a b)").bitcast(F32).rearrange("p (h j) -> p h j", j=128)
    catA = cat[:, 0:4, :].rearrange("p a (h j) -> p (a h) j", j=128)
    catB = cat[:, 4:8, :].rearrange("p a (h j) -> p (a h) j", j=128)
    KA = [("lscr", 0), ("lscr", 1)]
    KB = [("hid", f) for f in range(8)]
    pdma(lambda e: e.dma_start(out=stA, in_=w_spatial.rearrange("l h i j -> i (l h) j")), writes=KA)
    S.add("dve", lambda e: e.memset(stB, 0.0), writes=KB)
    wsub = w_spatial[:, :, 0:64, 0:64].rearrange("l h i j -> i (l h) j")
    pdma(lambda e: e.dma_start(out=stB[0:64, :, 0:64], in_=wsub), reads=KB, writes=["stB0"])
    pdma(lambda e: e.dma_start(out=stB[64:128, :, 64:128], in_=wsub), reads=KB, writes=["stB1"])
    S.add("act", lambda e: e.copy(out=catA, in_=stA), reads=KA, writes=[("cat", j) for j in range(4)])
    S.add("act", lambda e: e.copy(out=catB, in_=stB), reads=KB + ["stB0", "stB1"], writes=[("cat", j) for j in range(4, 8)] + KB)
    for var, src_bf, keys in ((0, catA, [("cat", j) for j in range(4)]), (1, catB, [("cat", j) for j in range(4, 8)])):
        for l in range(2):
            b = alloc1()
            for h in range(8):
                S.add("pe", lambda e, l=l, h=h, b=b, src_bf=src_bf: e.transpose(out=psb(b)[:, h, :], in_=src_bf[:, l * 8 + h, :],
                                                                               identity=idb[:]),
                      reads=keys + ["idb"], writes=[PS(b)])
            s0 = (l * 2 + var) * 8
            S.add("act", lambda e, b=b, s0=s0: e.copy(out=wsT[:, s0:s0 + 8, :], in_=psb(b)), reads=[PS(b)], writes=[("wsT", s0)])
            if var == 0:
                S.add("dve", lambda e, s0=s0: e.memset(wsT[64:128, s0:s0 + 8, 0:64], 0.0), writes=[("wsT", s0)])
    for l in range(2):
        bsrc = b_spatial[l].rearrange("(pr two) i -> two pr i", two=2)
        for hh in range(2):
            b0 = (l * 2) * 4
            pdma(lambda e, hh=hh, b0=b0, bsrc=bsrc: e.dma_start(out=bmat[hh * 64:(hh + 1) * 64, b0:b0 + 4, :],
                                                               in_=bsrc[hh].unsqueeze(0).broadcast_to([64, 4, 128])), writes=[("bmat", l, hh, 2)])
            b1 = (l * 2 + 1) * 4
            for half in range(2):
                pdma(lambda e, hh=hh, b1=b1, half=half, bsrc=bsrc: e.dma_start(
                    out=bmat[hh * 64:(hh + 1) * 64, b1:b1 + 4, half * 64:(half + 1) * 64],
                    in_=bsrc[hh][:, 0:64].unsqueeze(0).broadcast_to([64, 4, 64])), writes=[("bmat", l, hh, half)])
    stP = sscr[:, :, :].rearrange("p a (b j) -> p (a b) j", j=128)
    KP = [("sscr", 0), ("sscr", 1)]
    S.add("dve", lambda e: e.memset(stP[:, 0:4, :], 0.0), writes=KP)
    wpsrc = w_pool.rearrange("l (j gg) c e -> gg c (l j) e", gg=2)
    for gg in range(2):
        pdma(lambda e, gg=gg: e.dma_start(out=stP[gg * 64:(gg + 1) * 64, 0:4, gg * 64:(gg + 1) * 64], in_=wpsrc[gg]), reads=KP, writes=["stP%d" % gg])
    S.add("act", lambda e: e.copy(out=wpbd[:], in_=stP[:, 0:4, :]), reads=KP + ["stP0", "stP1"], writes=["wpbd"] + KP)

    def load_mix(l):
        S.add("sp", lambda e: e.dma_start(out=mixg[:], in_=ln_mix_g[l:l + 1, :].broadcast_to([128, D])),
              writes=["mixg"], dma_chan="pmg")
        S.add("sp", lambda e: e.dma_start(out=mixb[:], in_=ln_mix_b[l:l + 1, :].broadcast_to([128, D])),
              writes=["mixb"], dma_chan="pmb")

    def load_ffn(l):
        S.add("sp", lambda e: e.dma_start(out=ffng[:], in_=ln_ffn_g[l:l + 1, :].broadcast_to([128, D])),
              writes=["ffng"], dma_chan="pfg")
        S.add("sp", lambda e: e.dma_start(out=ffnb[:], in_=ln_ffn_b[l:l + 1, :].broadcast_to([128, D])),
              writes=["ffnb"], dma_chan="pfb")

    def load_vn(l):
        S.add("sp", lambda e: e.dma_start(out=vng[:], in_=v_norm_g[l:l + 1, :].broadcast_to([128, 512])),
              writes=["vng"], dma_chan="pvg")
        S.add("sp", lambda e: e.dma_start(out=vnb[:], in_=v_norm_b[l:l + 1, :].broadcast_to([128, 512])),
              writes=["vnb"], dma_chan="pvb")

    load_vn(0)
    for n in range(5, NSLOT):
        emit_load(n)
    st["ld"] = NSLOT
    load_mix(0)
    load_ffn(0)

    def transpose_chunk(c):
        b = alloc1()
        for k in range(8):
            S.add("pe", lambda e, k=k: e.transpose(out=psb(b)[:, k, :], in_=xbf[:, c, k * 128:(k + 1) * 128], identity=idb[:]),
                  reads=[("xbf", c), "idb"], writes=[PS(b)])
        S.add("act", lambda e: e.copy(out=xT[:, :, c * 128:(c + 1) * 128], in_=psb(b)), reads=[PS(b)], writes=[("xT", c)])

    def transposes(nch):
        for c in range(nch):
            transpose_chunk(c)

    def tile_layer(ti, l, base):
        kind, t = tiles[ti]
        sample = kind == "s"
        nch = 2 if sample else 4
        T = nch * 128
        nseg, seglen = (4, 64) if sample else (1, 512)
        var = 1 if sample else 0
        last_prompt = (kind == "p" and t == npt - 1)
        first_prompt = (kind == "p" and t == 0)
        XT_ALL = [("xT", c) for c in range(nch)]
        AE = "dve" if ti == 0 else "pool"
        LP, LC = nseg * (15 + seglen), nseg * (2 + seglen)

        def R(n):
            return ring[:, slot(base + n), :]

        def RK(n):
            return ("ring", slot(base + n))

        def xb_valid(j):
            return xbuf[:, j, 0:LP].rearrange("p (s q) -> p s q", s=nseg)[:, :, 15:]

        def z_valid(j, off=2):
            return zbuf[:, j, 0:LC].rearrange("p (s q) -> p s q", s=nseg)[:, :, off:off + seglen]

        def seg(ap2d):
            return ap2d.rearrange("p (s q) -> p s q", s=nseg)

        w0 = R(U_V0).rearrange("p (k n) -> p k n", k=4)
        w1 = R(U_V1).rearrange("p (k n) -> p k n", k=4)
        pend = st.pop("pendT", [])
        vbanks = []
        for c in range(nch):
            if c in pend:
                transpose_chunk(c)
            b = alloc1()
            vbanks.append(b)
            for kc in range(8):
                w = (w0 if kc < 4 else w1)[:, kc % 4, :]
                S.add("pe", lambda e, c=c, kc=kc, b=b, w=w: e.matmul(ps[:, b, :], lhsT=xT[:, kc, c * 128:(c + 1) * 128], rhs=w,
                                                                     start=(kc == 0), stop=(kc == 7)),
                      reads=[("xT", c), RK(U_V0 if kc < 4 else U_V1)], writes=[PS(b)])
            S.add("dve", lambda e, c=c, b=b: e.bn_stats(out=vst[:, c, :], in_=ps[:, b, :]), reads=[PS(b)], writes=[("vst", c)])
            S.add("dve", lambda e, c=c: e.bn_aggr(out=vmv[:, c, :], in_=vst[:, c, :]), reads=[("vst", c)], writes=[("vmv", c)])
        VMV = [("vmv", c) for c in range(nch)]
        S.add("act", lambda e: e.activation(out=vrs[:, 0:nch], in_=vmv[:, 0:nch, 1], func=AF.Sqrt, bias=epst[:, 0:1], scale=1.0),
              reads=VMV + ["eps"], writes=["vrs"])
        S.add("dve", lambda e: e.reciprocal(out=vrs[:, 0:nch], in_=vrs[:, 0:nch]), reads=["vrs"], writes=["vrs"])
        vo = lscr[:, 0, :].rearrange("p (a b) -> p a b", a=2)
        for c in range(nch):
            b = vbanks[c]
            S.add("dve", lambda e, c=c, b=b: e.scalar_tensor_tensor(out=vscr[:, c % 2, :], in0=ps[:, b, :], scalar=vmv[:, c, 0:1],
                                                                    in1=vng[:], op0=ALU.subtract, op1=ALU.mult),
                  reads=[PS(b), ("vmv", c), "vng"], writes=[("vscr", c % 2)])
            if sample:
                S.add("dve", lambda e, c=c: e.scalar_tensor_tensor(out=vo[:, c, :], in0=vscr[:, c % 2, :], scalar=vrs[:, c:c + 1],
                                                                   in1=vnb[:], op0=ALU.mult, op1=ALU.add),
                      reads=[("vscr", c % 2), "vrs", "vnb"], writes=[("lscr", 0)])
                S.add("act", lambda e, c=c: e.copy(out=vn[:, c, :], in_=vo[:, c, :]), reads=[("lscr", 0)], writes=[("vn", c)])
            else:
                S.add("dve", lambda e, c=c: e.scalar_tensor_tensor(out=vn[:, c, :], in0=vscr[:, c % 2, :], scalar=vrs[:, c:c + 1],
                                                                   in1=vnb[:], op0=ALU.mult, op1=ALU.add),
                      reads=[("vscr", c % 2), "vrs", "vnb"], writes=[("vn", c)])
        if sample:
            vo = lscr[:, 0, :].rearrange("p (a b) -> p a b", a=2)
            S.add("pool", lambda e, vo=vo: e.dma_start(out=o_v[l].rearrange("(c p) d -> p c d", p=128), in_=vo),
                  reads=[("lscr", 0)], dma_chan="ov")
        done(base + U_V0)
        done(base + U_V1)
        nxt_l = 1 - l
        if not (ti == len(tiles) - 1 and l == 1):
            load_vn(nxt_l)

        if sample:
            S.add("pool", lambda e: e.dma_start(out=stin[0:60, :], in_=spool[l]), writes=["stin"], dma_chan="si1")
            S.add("pool", lambda e: e.dma_start(out=stin2[0:8, :], in_=sconv[l]), writes=["stin2"], dma_chan="si2")
            for j in range(2):
                b = alloc1()
                S.add("pe", lambda e, j=j, b=b: e.matmul(ps[:, b, 0:60], lhsT=stin[0:60, j * 128:(j + 1) * 128], rhs=idf[0:60, 0:60],
                                                         start=True, stop=True),
                      reads=["stin", "idf"], writes=[PS(b)])
                S.add("act", lambda e, j=j, b=b: e.copy(out=xbuf[:, j, 0:LP].rearrange("p (s q) -> p s q", s=4)[:, :, 0:15],
                                                        in_=ps[:, b, 0:60].rearrange("p (s q) -> p s q", s=4)),
                      reads=[PS(b)], writes=[("xbuf", j)])
                b = alloc1()
                S.add("pe", lambda e, j=j, b=b: e.matmul(ps[:, b, 0:8], lhsT=stin2[0:8, j * 128:(j + 1) * 128], rhs=idf[0:8, 0:8],
                                                         start=True, stop=True),
                      reads=["stin2", "idf"], writes=[PS(b)])
                S.add("act", lambda e, j=j, b=b: e.copy(out=zbuf[:, j, 0:LC].rearrange("p (s q) -> p s q", s=4)[:, :, 0:2],
                                                        in_=ps[:, b, 0:8].rearrange("p (s q) -> p s q", s=4)),
                      reads=[PS(b)], writes=[("zbuf", j)])
        else:
            S.add("act", lambda e: e.copy(out=xbuf[:, :, 0:15], in_=ptail[:, l, :, :]), reads=["ptail%d" % l],
                  writes=[("xbuf", 0), ("xbuf", 1)])
            S.add("act", lambda e: e.copy(out=zbuf[:, :, 0:2], in_=ctail[:, l, :, :]), reads=["ctail%d" % l],
                  writes=[("zbuf", 0), ("zbuf", 1)])

        def fm_group(un, j, rhs_all=True):
            w = R(un).rearrange("p (k n) -> p k n", k=8)
            b = alloc1()
            for kc in range(8):
                S.add("pe", lambda e, kc=kc, b=b, w=w: e.matmul(ps[:, b, 0:T], lhsT=w[:, kc, j * 128:(j + 1) * 128], rhs=xT[:, kc, 0:T],
                                                                start=(kc == 0), stop=(kc == 7)),
                      reads=XT_ALL + [RK(un)], writes=[PS(b)])
            return b

        for j in range(2):
            b = fm_group(U_XB, j)
            S.add("act", lambda e, j=j, b=b: e.copy(out=xb_valid(j), in_=seg(ps[:, b, 0:T])), reads=[PS(b)], writes=[("xbuf", j)])
        if not sample:
            S.add("act", lambda e: e.copy(out=ptail[:, l, :, :], in_=xbuf[:, :, T:T + 15]), reads=[("xbuf", 0), ("xbuf", 1)],
                  writes=["ptail%d" % l])
        for j in range(2):
            b = fm_group(U_GC, j)
            S.add("act", lambda e, j=j, b=b: e.copy(out=z_valid(j), in_=seg(ps[:, b, 0:T])), reads=[PS(b)], writes=[("zbuf", j)])
        for j in range(2):
            b = fm_group(U_H, j)
            S.add("dve", lambda e, j=j, b=b: e.tensor_tensor(out=z_valid(j), in0=z_valid(j), in1=seg(ps[:, b, 0:T]), op=ALU.mult),
                  reads=[PS(b), ("zbuf", j)], writes=[("zbuf", j)])
        if not sample:
            S.add("act", lambda e: e.copy(out=ctail[:, l, :, :], in_=zbuf[:, :, T:T + 2]), reads=[("zbuf", 0), ("zbuf", 1)],
                  writes=["ctail%d" % l])

        st_chunks = list(range(nch)) if sample else ([nch - 1] if last_prompt else [])
        wxb = R(U_XB).rearrange("p (k n) -> p k n", k=8)
        wgc = R(U_GC).rearrange("p (k n) -> p k n", k=8)
        wh = R(U_H).rearrange("p (k n) -> p k n", k=8)
        for c in st_chunks:
            b = alloc1()
            for kc in range(8):
                S.add("pe", lambda e, c=c, kc=kc, b=b: e.matmul(ps[:, b, 0:256], lhsT=xT[:, kc, c * 128:(c + 1) * 128], rhs=wxb[:, kc, :],
                                                                start=(kc == 0), stop=(kc == 7)),
                      reads=[("xT", c), RK(U_XB)], writes=[PS(b)])
            S.add("act", lambda e, c=c, b=b: e.copy(out=sttm[:, 0, :], in_=ps[:, b, 0:256]), reads=[PS(b)], writes=[("sttm", 0)])
            b2 = alloc1()
            for wi, ww in enumerate((wgc, wh)):
                for kc in range(8):
                    S.add("pe", lambda e, c=c, kc=kc, b2=b2, wi=wi, ww=ww: e.matmul(
                        ps[:, b2, wi * 256:(wi + 1) * 256], lhsT=xT[:, kc, c * 128:(c + 1) * 128], rhs=ww[:, kc, :],
                        start=(kc == 0), stop=(kc == 7)),
                        reads=[("xT", c), RK(U_GC if wi == 0 else U_H)], writes=[PS(b2)])
            S.add("act", lambda e, c=c, b2=b2: e.copy(out=sttm2[:, 0, :], in_=ps[:, b2, 0:256]), reads=[PS(b2)],
                  writes=[("sttm2", 0)])
            S.add("dve", lambda e, c=c, b2=b2: e.tensor_tensor(out=sttm2[:, 0, :], in0=sttm2[:, 0, :], in1=ps[:, b2, 256:512],
                                                               op=ALU.mult),
                  reads=[PS(b2), ("sttm2", 0)], writes=[("sttm2", 0)])
            if sample:
                for hf in range(2):
                    q = 2 * c + hf
                    S.add("pool", lambda e, c=c, hf=hf, q=q: e.dma_start(out=o_pool_s[l, q * 15:(q + 1) * 15, :],
                                                                         in_=sttm[hf * 64 + 49:hf * 64 + 64, 0, :]),
                          reads=[("sttm", 0)], dma_chan=("so", 0))
                    S.add("pool", lambda e, c=c, hf=hf, q=q: e.dma_start(out=o_conv_s[l, q * 2:(q + 1) * 2, :],
                                                                         in_=sttm2[hf * 64 + 62:hf * 64 + 64, 0, :]),
                          reads=[("sttm2", 0)], dma_chan=("so", 1))
            else:
                S.add("pool", lambda e, c=c: e.dma_start(out=o_pool_p[l], in_=sttm[113:128, 0, :]),
                      reads=[("sttm", 0)], dma_chan=("so", 2))
                S.add("pool", lambda e, c=c: e.dma_start(out=o_conv_p[l], in_=sttm2[126:128, 0, :]),
                      reads=[("sttm2", 0)], dma_chan=("so", 3))
        done(base + U_XB)
        done(base + U_GC)
        done(base + U_H)

        rw = cst_sb[:, 0:2]
        msk = cst_sb[:, 2:4]
        for j in range(2):
            X = xbuf[:, j, :]
            A = pscr[:, 0, :]
            B = pscr[:, 1, :]
            XK, AK, BK = ("xbuf", j), ("pscr", 0), ("pscr", 1)
            S.add(AE, lambda e, X=X, A=A: e.tensor_tensor(out=A[:, 1:LP], in0=X[:, 1:LP], in1=X[:, 0:LP - 1], op=ALU.add),
                  reads=[XK], writes=[AK])
            if j == 0:
                S.add("dve", lambda e, A=A, B=B: e.scalar_tensor_tensor(out=B[:, 3:LP], in0=A[:, 1:LP - 2], scalar=msk[:, 0:1],
                                                                        in1=A[:, 3:LP], op0=ALU.mult, op1=ALU.add),
                      reads=[AK, "cst"], writes=[BK])
            else:
                S.add(AE, lambda e, A=A, B=B: e.tensor_tensor(out=B[:, 3:LP], in0=A[:, 3:LP], in1=A[:, 1:LP - 2], op=ALU.add),
                      reads=[AK], writes=[BK])
                S.add(AE, lambda e, A=A, B=B: e.tensor_tensor(out=A[:, 7:LP], in0=B[:, 7:LP], in1=B[:, 3:LP - 4], op=ALU.add),
                      reads=[BK], writes=[AK])
                S.add("dve", lambda e, A=A, B=B: e.scalar_tensor_tensor(out=B[:, 15:LP], in0=A[:, 7:LP - 8], scalar=msk[:, 1:2],
                                                                        in1=A[:, 15:LP], op0=ALU.mult, op1=ALU.add),
                      reads=[AK, "cst"], writes=[BK])
            Bv = B[:, 0:LP].rearrange("p (s q) -> p s q", s=nseg)[:, :, 15:]
            S.add("dve", lambda e, j=j, Bv=Bv: e.scalar_tensor_tensor(out=seg(diff[:, j, 0:T]), in0=Bv, scalar=rw[:, j:j + 1],
                                                                      in1=xb_valid(j), op0=ALU.mult, op1=ALU.subtract),
                  reads=[BK, XK, "cst"], writes=[("diff", j)])
            if first_prompt:
                rc = cst_sb[:, 4 + 15 * j:4 + 15 * (j + 1)]
                S.add("dve", lambda e, A=A, B=B, rc=rc: e.tensor_tensor(out=A[:, 0:15], in0=B[:, 15:30], in1=rc, op=ALU.mult),
                      reads=[BK, "cst"], writes=[AK])
                S.add("dve", lambda e, j=j, A=A, X=X: e.tensor_tensor(out=diff[:, j, 0:15], in0=A[:, 0:15], in1=X[:, 15:30], op=ALU.subtract),
                      reads=[AK, XK], writes=[("diff", j)])
        for j in range(2):
            ZK, YK = ("zbuf", j), ("ybuf", j)
            yv = seg(ybuf[:, j, 0:T])
            S.add("act", lambda e, j=j, yv=yv: e.activation(out=yv, in_=z_valid(j, 0), func=AF.Copy, scale=cw[:, l, j, 0:1]),
                  reads=[ZK, ("cw", l, 0), ("cw", l, 1), ("cw", l, 2)], writes=[YK])
            for k in (1, 2):
                tv = seg(ctmp[:, k - 1, 0:T])
                S.add("act", lambda e, j=j, tv=tv, k=k: e.activation(out=tv, in_=z_valid(j, k), func=AF.Copy, scale=cw[:, l, j, k:k + 1]),
                      reads=[ZK, ("cw", l, 0), ("cw", l, 1), ("cw", l, 2)], writes=[("ctmp", k)])
                S.add(AE, lambda e, yv=yv, tv=tv: e.tensor_tensor(out=yv, in0=yv, in1=tv, op=ALU.add),
                      reads=[YK, ("ctmp", k)], writes=[YK])

        for j in range(4):
            bz = alloc1()
            for c in range(nch):
                for hh in range(2):
                    h = 2 * j + hh
                    si = (l * 2 + var) * 8 + h
                    S.add("pe", lambda e, c=c, hh=hh, h=h, si=si, bz=bz: e.matmul(
                        ps[hh * 64:(hh + 1) * 64, bz, c * 128:(c + 1) * 128], lhsT=vn[:, c, h * 64:(h + 1) * 64], rhs=wsT[:, si, :],
                        start=True, stop=True),
                        reads=[("vn", c), ("wsT", (si // 8) * 8)], writes=[PS(bz)])
            un = U_U01 if j < 2 else U_U23
            bu = fm_group(un, j % 2)
            bi = (l * 2 + var) * 4 + j
            S.add("dve", lambda e, j=j, bz=bz, bi=bi: e.tensor_tensor(
                out=sscr[:, j % 2, 0:T].rearrange("p (c i) -> p c i", c=nch),
                in0=ps[:, bz, 0:T].rearrange("p (c i) -> p c i", c=nch),
                in1=bmat[:, bi:bi + 1, :].broadcast_to([128, nch, 128]), op=ALU.add),
                reads=[PS(bz)] + [("bmat", l, a_, b_) for a_ in range(2) for b_ in range(3)], writes=[("sscr", j % 2)])
            S.add("dve", lambda e, j=j, bu=bu: e.tensor_tensor(out=cat[:, j, 0:T], in0=sscr[:, j % 2, 0:T], in1=ps[:, bu, 0:T], op=ALU.mult),
                  reads=[PS(bu), ("sscr", j % 2)], writes=[("cat", j)])
            if j == 1:
                done(base + U_U01)
        done(base + U_U23)

        for j in range(2):
            b = alloc1()
            S.add("pe", lambda e, j=j, b=b: e.matmul(ps[:, b, 0:T], lhsT=wpbd[:, l * 2 + j, :], rhs=diff[:, j, 0:T], start=True, stop=True),
                  reads=[("diff", j), "wpbd"], writes=[PS(b)])
            S.add("act", lambda e, j=j, b=b: e.activation(out=cat[:, 4 + j, 0:T], in_=ps[:, b, 0:T], func=AF.Identity,
                                                          scale=pscale[:, l, j:j + 1]),
                  reads=[PS(b), ("pscale", l)], writes=[("cat", 4 + j)])
        for j in range(2):
            b = fm_group(U_GB, j)
            S.add("dve", lambda e, j=j, b=b: e.tensor_tensor(out=cat[:, 6 + j, 0:T], in0=ybuf[:, j, 0:T], in1=ps[:, b, 0:T], op=ALU.mult),
                  reads=[PS(b), ("ybuf", j)], writes=[("cat", 6 + j)])
        done(base + U_GB)

        def tm_proj_ln(un0, nk, src, src_keys, gam, bet, gk, bk, make_bf, store_out):
            def wk(k):
                return R(un0 + k // 2).rearrange("p (k n) -> p k n", k=2)[:, k % 2, :]

            pbs = {}

            def stage_a(c):
                pb = pbs[c]
                XK = ("xres", c)
                S.add("dve", lambda e: e.scalar_tensor_tensor(out=xres[:, c, :], in0=xres[:, c, :], scalar=ALPHA,
                                                              in1=ps[:, pb:pb + 2, :].rearrange("p a b -> p (a b)"),
                                                              op0=ALU.mult, op1=ALU.add, accum_out=lsum[:, c:c + 1]),
                      reads=[PS(pb), PS(pb + 1), XK], writes=[XK, ("lsum", c)])
                S.add("act", lambda e: e.activation(out=xbf[:, c, :], in_=xres[:, c, :], func=AF.Square, accum_out=lsq[:, c:c + 1]),
                      reads=[XK], writes=[("xbf", c), ("lsq", c)])
                S.add("dve", lambda e: e.tensor_scalar_mul(out=lmean[:, c:c + 1], in0=lsum[:, c:c + 1], scalar1=1.0 / D),
                      reads=[("lsum", c)], writes=[("lmean", c)])
                S.add("dve", lambda e: e.scalar_tensor_tensor(out=lnb[:, c:c + 1], in0=lmean[:, c:c + 1], scalar=-1.0, in1=lmean[:, c:c + 1],
                                                              op0=ALU.mult, op1=ALU.mult),
                      reads=[("lmean", c)], writes=[("lnb", c)])
                S.add("dve", lambda e: e.tensor_scalar_add(out=lnb[:, c:c + 1], in0=lnb[:, c:c + 1], scalar1=EPS),
                      reads=[("lnb", c)], writes=[("lnb", c)])
                S.add("dve", lambda e: e.scalar_tensor_tensor(out=lscr[:, c % 2, :], in0=xres[:, c, :], scalar=lmean[:, c:c + 1], in1=gam[:],
                                                              op0=ALU.subtract, op1=ALU.mult),
                      reads=[XK, ("lmean", c), gk], writes=[("lscr", c % 2)])
                S.add("act", lambda e: e.activation(out=lrs[:, c:c + 1], in_=lsq[:, c:c + 1], func=AF.Sqrt, scale=1.0 / D, bias=lnb[:, c:c + 1]),
                      reads=[("lsq", c), ("lnb", c)], writes=[("lrs", c)])

            def stage_b(c):
                XK = ("xres", c)
                S.add("dve", lambda e: e.reciprocal(out=lrs[:, c:c + 1], in_=lrs[:, c:c + 1]), reads=[("lrs", c)], writes=[("lrs", c)])
                S.add("dve", lambda e: e.scalar_tensor_tensor(out=xres[:, c, :], in0=lscr[:, c % 2, :], scalar=lrs[:, c:c + 1], in1=bet[:],
                                                              op0=ALU.mult, op1=ALU.add),
                      reads=[("lscr", c % 2), ("lrs", c), bk], writes=[XK])
                if make_bf:
                    S.add("act", lambda e: e.copy(out=xbf[:, c, :], in_=xres[:, c, :]), reads=[XK], writes=[("xbf", c)])
                if store_out is not None:
                    S.add("pool", lambda e: e.dma_start(out=store_out[c * 128:(c + 1) * 128, :], in_=xres[:, c, :]),
                          reads=[XK], dma_chan=("ys", c))

            def mm(c, mid_at=None, mid_hook=None):
                pb = pbs[c] = alloc2()
                for k in range(nk):
                    if mid_at is not None and k == mid_at:
                        mid_hook()
                    w = wk(k)
                    for hf in range(2):
                        S.add("pe", lambda e, k=k, hf=hf, w=w: e.matmul(ps[:, pb + hf, :], lhsT=src[:, k, c * 128:(c + 1) * 128],
                                                                        rhs=w[:, hf * 512:(hf + 1) * 512], start=(k == 0), stop=(k == nk - 1)),
                              reads=[src_keys[k], RK(un0 + k // 2)], writes=[PS(pb), PS(pb + 1)])
                    if c == nch - 1 and k % 2 == 1:
                        done(base + un0 + k // 2)

            pend_out = [nch - 2, nch - 1] if make_bf else []
            if nch == 4 and nk > 8:
                for c in range(3):
                    mm(c); stage_a(c); stage_b(c)
                if make_bf:
                    transpose_chunk(0)
                    transpose_chunk(1)
                    mm(3, mid_at=16, mid_hook=lambda: transpose_chunk(2))
                    pend_out = [3]
                else:
                    mm(3)
                stage_a(3); stage_b(3)
            elif nch == 4:
                mm(0); stage_a(0)
                mm(1); stage_a(1); stage_b(0)
                mm(2); stage_b(1)
                mm(3)
                if make_bf:
                    transpose_chunk(0)
                    transpose_chunk(1)
                stage_a(2); stage_a(3); stage_b(2); stage_b(3)
            else:
                for c in range(nch):
                    mm(c)
                    stage_a(c)
                    stage_b(c)
            return pend_out

        CAT_ALL = [("cat", j) for j in range(8)]
        pend1 = tm_proj_ln(U_OUT0, 8, cat, CAT_ALL, mixg, mixb, "mixg", "mixb", True, None)
        if not (ti == len(tiles) - 1 and l == 1):
            load_mix(nxt_l)

        def gu_block(g, fi, nf, c0, c1):
            f = 4 * g + fi
            t0, t1 = c0 * 128, c1 * 128
            xk = [("xT", c) for c in range(c0, c1)]
            banks = []
            for wi in range(2):
                b = alloc1()
                banks.append(b)
                for kc in range(8):
                    w = R(U_GU0 + 4 * g + kc // 2)[:, wi * 2 * nf * 128:(wi + 1) * 2 * nf * 128].rearrange("p (k n) -> p k n", k=2)
                    S.add("pe", lambda e, kc=kc, b=b, w=w: e.matmul(ps[:, b, t0:t1], lhsT=w[:, kc % 2, fi * 128:(fi + 1) * 128],
                                                                    rhs=xT[:, kc, t0:t1], start=(kc == 0), stop=(kc == 7)),
                          reads=xk + [RK(U_GU0 + 4 * g + kc // 2)], writes=[PS(b)])
            S.add("act", lambda e: e.activation(out=sscr[:, f % 2, t0:t1], in_=ps[:, banks[0], t0:t1], func=AF.Silu),
                  reads=[PS(banks[0])], writes=[("sscr", f % 2)])
            S.add("dve", lambda e: e.tensor_tensor(out=hid[:, f, t0:t1], in0=sscr[:, f % 2, t0:t1], in1=ps[:, banks[1], t0:t1], op=ALU.mult),
                  reads=[PS(banks[1]), ("sscr", f % 2)], writes=[("hid", f)])

        for g in range(6):
            nf = 4 if g < 5 else 2
            if g == 0 and nch == 4:
                for fi in range(nf):
                    gu_block(g, fi, nf, 0, 2)
                for c in pend1:
                    transpose_chunk(c)
                for fi in range(nf):
                    gu_block(g, fi, nf, 2, 4)
            else:
                if g == 0:
                    for c in pend1:
                        transpose_chunk(c)
                for fi in range(nf):
                    gu_block(g, fi, nf, 0, nch)
            for j in range(4):
                done(base + U_GU0 + 4 * g + j)
            if g == 0 and l == 1 and ti + 1 < len(tiles):
                load_xbf(ti + 1)

        if l == 1 and ti + 1 < len(tiles):
            transposes(tile_src(ti + 1)[2])

        HID_ALL = [("hid", f) for f in range(NFC)]
        _, ydst, _ = tile_src(ti)
        pend2 = tm_proj_ln(U_DN0, NFC, hid, HID_ALL, ffng, ffnb, "ffng", "ffnb", l == 0, ydst if l == 1 else None)
        if not (ti == len(tiles) - 1 and l == 1):
            load_ffn(nxt_l)
        if l == 0:
            st["pendT"] = pend2
        elif ti + 1 < len(tiles):
            load_xres(ti + 1)

    transposes(tile_src(0)[2])
    for ti in range(len(tiles)):
        for l in range(2):
            tile_layer(ti, l, (ti * 2 + l) * NUNIT)
    with nc.allow_non_contiguous_dma(reason="tiny per-partition parameter loads"):
        S.emit(final_wait_eng="sp")
    return nc


def _consts():
    cst = np.zeros((128, 34), np.float32)
    p = np.arange(128)
    up = (p >= 64)
    win = np.zeros((128, 2), np.float32)
    win[:, 0] = np.where(up, 4, 2)
    win[:, 1] = np.where(up, 16, 8)
    cst[:, 0:2] = 1.0 / win
    cst[:, 2:4] = up[:, None].astype(np.float32)
    for j in range(2):
        for t in range(15):
            cst[:, 4 + 15 * j + t] = 1.0 / np.minimum(t + 1, win[:, j])
    return cst


_WNAMES = ["ln_mix_g", "ln_mix_b", "ln_ffn_g", "ln_ffn_b", "v_norm_g", "v_norm_b", "w_in", "w_spatial", "b_spatial",
           "w_pool", "pool_scale", "conv_w", "w_out", "w_gate", "w_up", "w_down"]


def make_in_maps(inputs, n_cores, npt):
    f = lambda a: np.ascontiguousarray(np.asarray(a, dtype=np.float32))
    shared = {k: f(inputs[k]) for k in _WNAMES}
    shared["ident"] = np.eye(128, dtype=np.float32)
    shared["cst"] = _consts()
    xp = f(inputs["x_prompt"]); xs = f(inputs["x_sample"])
    sp = f(inputs["state_pool"]); sc = f(inputs["state_conv"])
    maps = []
    for i in range(n_cores):
        m = dict(shared)
        m["xp"] = np.ascontiguousarray(xp[i, :npt * 512])
        m["xs"] = np.ascontiguousarray(xs[4 * i:4 * i + 4].reshape(256, D))
        m["spool"] = np.ascontiguousarray(sp[:, 4 * i:4 * i + 4].reshape(2, 60, 256))
        m["sconv"] = np.ascontiguousarray(sc[:, 4 * i:4 * i + 4].reshape(2, 8, 256))
        maps.append(m)
    return maps


def gather(results, n_cores, npt):
    y_p = np.stack([r["yp"] for r in results], 0)
    y_s = np.concatenate([r["ys"].reshape(4, 64, D) for r in results], 0)
    pp = np.stack([r["o_pool_p"] for r in results], 1)
    pss = np.concatenate([r["o_pool_s"].reshape(2, 4, 15, 256) for r in results], 1)
    cp = np.stack([r["o_conv_p"] for r in results], 1)
    cs = np.concatenate([r["o_conv_s"].reshape(2, 4, 2, 256) for r in results], 1)
    vs = np.concatenate([r["o_v"].reshape(2, 4, 64, 512) for r in results], 1)
    return tuple(np.ascontiguousarray(a.astype(np.float32)) for a in (y_p, y_s, pp, pss, cp, cs, vs))


def kernel(**inputs):
    nc = build(NPT_FULL)
    maps = make_in_maps(inputs, N_CORES, NPT_FULL)
    res = run_bass_kernel_spmd(nc, maps, core_ids=list(range(N_CORES)))
    return gather(res.results, N_CORES, NPT_FULL)
```

```python
import contextlib
import numpy as np
import concourse.bass as bass
import concourse.mybir as mybir
from concourse.bass_utils import run_bass_kernel_spmd

F32 = mybir.dt.float32
BF16 = mybir.dt.bfloat16
AF = mybir.ActivationFunctionType
ALU = mybir.AluOpType

D = 1024
DFF = 2816
NFC = 22
ALPHA = float(4 ** 0.25)
EPS = 1e-5
NSLOT = 15
NUNIT = 47
U_V0, U_V1, U_XB, U_GC, U_H, U_U01, U_U23, U_GB, U_OUT0, U_GU0, U_DN0 = 0, 1, 2, 3, 4, 5, 6, 7, 8, 12, 36
N_CORES = 8
NPT_FULL = 16


class _Op:
    __slots__ = ("eng", "fn", "deps", "is_dma", "chan", "sig", "idx", "waits", "vc")

    def __init__(self, eng, fn, is_dma, chan):
        self.eng = eng
        self.fn = fn
        self.deps = []
        self.is_dma = is_dma
        self.chan = chan
        self.sig = 0
        self.idx = 0


class Sched:
    ENGS = ("pe", "act", "dve", "pool", "sp")

    def __init__(self, nc):
        self.nc = nc
        self.ops = {e: [] for e in self.ENGS}
        self.res = {}
        self.chan_count = {}
        self.n = 0

    def add(self, eng, fn, reads=(), writes=(), dma_chan=None):
        op = _Op(eng, fn, dma_chan is not None, dma_chan)
        op.idx = self.n
        self.n += 1
        deps = {}
        res = self.res

        def dep(o, raw):
            if (not o.is_dma) and (not op.is_dma) and o.eng == op.eng and op.eng == "pe":
                return
            deps[o.idx] = o

        for k in reads:
            st = res.get(k)
            if st is None:
                st = res[k] = [{}, {}]
            for o in st[0].values():
                dep(o, True)
        for k in writes:
            st = res.get(k)
            if st is None:
                st = res[k] = [{}, {}]
            for o in st[1].values():
                dep(o, False)
            for o in st[0].values():
                dep(o, False)
        tok = ("dma", dma_chan) if op.is_dma else eng
        for k in reads:
            res[k][1][tok] = op
        for k in writes:
            st = res[k]
            if st[1]:
                st[0] = {tok: op}
                st[1] = {}
            else:
                st[0][tok] = op
        op.deps = list(deps.values())
        if op.is_dma:
            c = self.chan_count.get(dma_chan, 0) + 1
            self.chan_count[dma_chan] = c
            op.sig = c
        self.ops[eng].append(op)
        return op

    def _plan_waits(self):
        allops = sorted((op for e in self.ENGS for op in self.ops[e]), key=lambda o: o.idx)
        K = {e: {} for e in self.ENGS}
        nsig = {e: 0 for e in self.ENGS}
        for op in allops:
            Ke = K[op.eng]
            need = {}
            for d in op.deps:
                if d.is_dma:
                    key, val = ("c", d.chan), 16 * d.sig
                else:
                    key, val = ("e", d.eng), d.sig
                cur = need.get(key)
                if cur is None or cur[0] < val:
                    need[key] = (val, d)
            todo = []
            for key, (val, d) in sorted(need.items(), key=lambda kv: -kv[1][1].idx):
                if Ke.get(key, 0) >= val:
                    continue
                todo.append((key, val))
                for k2, v2 in d.vc.items():
                    if Ke.get(k2, 0) < v2:
                        Ke[k2] = v2
                if Ke.get(key, 0) < val:
                    Ke[key] = val
            op.waits = todo
            if op.is_dma:
                vc = dict(Ke)
                vc[("c", op.chan)] = 16 * op.sig
                op.vc = vc
            else:
                if op.sig > 0:
                    nsig[op.eng] = op.sig
                if op.sig > 0:
                    vc = dict(Ke)
                    if vc.get(("e", op.eng), 0) < op.sig:
                        vc[("e", op.eng)] = op.sig
                    op.vc = vc
                else:
                    op.vc = None

    def emit(self, final_wait_eng="sp"):
        nc = self.nc
        for e in self.ENGS:
            for op in self.ops[e]:
                for d in op.deps:
                    if not d.is_dma and d.sig == 0:
                        d.sig = -1
        for e in self.ENGS:
            c = 0
            for op in self.ops[e]:
                if not op.is_dma and op.sig == -1:
                    c += 1
                    op.sig = c
        self._plan_waits()
        chans = sorted(self.chan_count.keys(), key=str)
        with contextlib.ExitStack() as st:
            esem = {e: st.enter_context(nc.semaphore("s_" + e)) for e in self.ENGS}
            csem = {}
            for i, c in enumerate(chans):
                csem[c] = st.enter_context(nc.semaphore("c%d" % i))
            block = st.enter_context(nc.Block())
            engobj = {"pe": block.tensor, "act": block.scalar, "dve": block.vector,
                      "pool": block.gpsimd, "sp": block.sync}

            def semof(key):
                return csem[key[1]] if key[0] == "c" else esem[key[1]]

            def make(e):
                def body(eng):
                    for op in self.ops[e]:
                        todo = op.waits
                        for key, val in todo[:-1]:
                            eng.wait_ge(semof(key), val)
                        inst = op.fn(eng)
                        if todo:
                            inst._wait_ge(semof(todo[-1][0]), todo[-1][1])
                        if op.is_dma:
                            inst.then_inc(csem[op.chan], 16)
                        elif op.sig > 0:
                            inst.then_inc(esem[op.eng], 1)
                    if e == final_wait_eng:
                        for c in chans:
                            eng.wait_ge(csem[c], 16 * self.chan_count[c])
                return body

            for e in self.ENGS:
                if self.ops[e] or e == final_wait_eng:
                    engobj[e](make(e))


def build(npt, with_sample=True):
    nc = bass.Bass("TRN2", target_bir_lowering=False)
    NP = npt * 512

    def din(name, shape):
        return nc.dram_tensor(name, shape, F32, kind="ExternalInput").ap()

    def dout(name, shape):
        return nc.dram_tensor(name, shape, F32, kind="ExternalOutput").ap()

    xp = din("xp", [NP, D])
    xs = din("xs", [256, D])
    spool = din("spool", [2, 60, 256])
    sconv = din("sconv", [2, 8, 256])
    ln_mix_g = din("ln_mix_g", [2, D]); ln_mix_b = din("ln_mix_b", [2, D])
    ln_ffn_g = din("ln_ffn_g", [2, D]); ln_ffn_b = din("ln_ffn_b", [2, D])
    v_norm_g = din("v_norm_g", [2, 512]); v_norm_b = din("v_norm_b", [2, 512])
    w_in = din("w_in", [2, D, 2048])
    w_spatial = din("w_spatial", [2, 8, 128, 128])
    b_spatial = din("b_spatial", [2, 8, 128])
    w_pool = din("w_pool", [2, 4, 64, 64])
    pool_scale = din("pool_scale", [2, 256])
    conv_w = din("conv_w", [2, 3, 256])
    w_out = din("w_out", [2, D, D])
    w_gate = din("w_gate", [2, D, DFF]); w_up = din("w_up", [2, D, DFF])
    w_down = din("w_down", [2, DFF, D])
    ident = din("ident", [128, 128])
    cst = din("cst", [128, 34])

    yp = dout("yp", [NP, D])
    ys = dout("ys", [256, D])
    o_pool_p = dout("o_pool_p", [2, 15, 256])
    o_pool_s = dout("o_pool_s", [2, 60, 256])
    o_conv_p = dout("o_conv_p", [2, 2, 256])
    o_conv_s = dout("o_conv_s", [2, 8, 256])
    o_v = dout("o_v", [2, 256, 512])

    wbf = nc.dram_tensor("wbf", [2, NUNIT, 128, 2048], BF16).ap()

    sb = nc.alloc_sbuf_tensor
    ring = sb("ring", [128, NSLOT, 2048], BF16)
    xres = sb("xres", [128, 4, D], F32)
    xbf = sb("xbf", [128, 4, D], BF16)
    xT = sb("xT", [128, 8, 512], BF16)
    vscr = sb("vscr", [128, 2, 512], F32)
    vn = sb("vn", [128, 4, 512], BF16)
    xbuf = sb("xbuf", [128, 2, 528], F32)
    pscr = sb("pscr", [128, 2, 528], F32)
    zbuf = sb("zbuf", [128, 2, 516], F32)
    ybuf = sb("ybuf", [128, 2, 512], F32)
    diff = sb("diff", [128, 2, 512], BF16)
    cat = sb("cat", [128, 8, 512], BF16)
    hid = sb("hid", [128, NFC, 512], BF16)
    sscr = sb("sscr", [128, 2, 512], F32)
    lscr = sb("lscr", [128, 2, D], F32)
    mixg = sb("mixg", [128, D], F32); mixb = sb("mixb", [128, D], F32)
    ffng = sb("ffng", [128, D], F32); ffnb = sb("ffnb", [128, D], F32)
    vng = sb("vng", [128, 512], F32); vnb = sb("vnb", [128, 512], F32)
    wsT = sb("wsT", [128, 32, 128], BF16)
    wpbd = sb("wpbd", [128, 4, 128], BF16)
    bmat = sb("bmat", [128, 16, 128], F32)
    ctmp = sb("ctmp", [128, 2, 512], F32)
    idf = sb("idf", [128, 128], F32)
    idb = sb("idb", [128, 128], BF16)
    cst_sb = sb("cst_sb", [128, 34], F32)
    pscale = sb("pscale", [128, 2, 2], F32)
    cw = sb("cw", [128, 2, 2, 3], F32)
    epst = sb("epst", [128, 1], F32)
    ptail = sb("ptail", [128, 2, 2, 15], F32)
    ctail = sb("ctail", [128, 2, 2, 2], F32)
    stin = sb("stin", [64, 256], F32)
    stin2 = sb("stin2", [8, 256], F32)
    sttm = sb("sttm", [128, 1, 256], F32)
    sttm2 = sb("sttm2", [128, 1, 256], F32)
    vst = sb("vst", [128, 4, 6], F32); vmv = sb("vmv", [128, 4, 2], F32); vrs = sb("vrs", [128, 4], F32)
    lsum = sb("lsum", [128, 4], F32); lmean = sb("lmean", [128, 4], F32)
    lnb = sb("lnb", [128, 4], F32); lsq = sb("lsq", [128, 4], F32); lrs = sb("lrs", [128, 4], F32)
    ps = nc.alloc_psum_tensor("ps", [128, 8, 512], F32)

    S = Sched(nc)
    st = {"pb": 0, "cv": 0, "ld": 0, "io": 0}

    def alloc1():
        b = st["pb"] % 8
        st["pb"] += 1
        return b

    def alloc2():
        if st["pb"] % 2:
            st["pb"] += 1
        b = st["pb"] % 8
        st["pb"] += 2
        return b

    def PS(b):
        return ("ps", b)

    def psb(b):
        return ps[:, b, :].bitcast(BF16).rearrange("p (k n) -> p k n", k=8)

    UL = [("in", 512, 0, 4), ("in", 512, 4, 4)]
    for c0 in (1024, 1536, 1792, 0, 256, 1280):
        UL.append(("inF", c0))
    for j in range(4):
        UL.append(("out", j))
    for g in range(6):
        for j in range(4):
            UL.append(("gu", g, j))
    for j in range(11):
        UL.append(("down", j))
    assert len(UL) == NUNIT

    def unit_keys(l, ui):
        if UL[ui][0] == "gu":
            return [("wbf", l, ui, 0), ("wbf", l, ui, 1)]
        return [("wbf", l, ui, 0)]

    def emit_conv(l, ui):
        k = UL[ui]
        dst = wbf[l, ui]
        jobs = []
        if k[0] == "in":
            _, c0, kc0, nk = k
            src = w_in[l, kc0 * 128:(kc0 + nk) * 128, c0:c0 + 512].rearrange("(k p) n -> p k n", p=128)
            jobs.append((dst.rearrange("p (k n) -> p k n", k=nk), src, 0))
        elif k[0] == "inF":
            c0 = k[1]
            src = w_in[l, :, c0:c0 + 256].rearrange("(k p) n -> p k n", p=128)
            jobs.append((dst.rearrange("p (k n) -> p k n", k=8), src, 0))
        elif k[0] == "out":
            j = k[1]
            src = w_out[l, j * 256:(j + 1) * 256, :].rearrange("(k p) n -> p k n", p=128)
            jobs.append((dst.rearrange("p (k n) -> p k n", k=2), src, 0))
        elif k[0] == "gu":
            _, g, j = k
            nf = 4 if g < 5 else 2
            f0 = 4 * g
            for wi, wsrc in enumerate((w_gate, w_up)):
                src = wsrc[l, j * 256:(j + 1) * 256, f0 * 128:(f0 + nf) * 128].rearrange("(k p) n -> p k n", p=128)
                d = dst[:, wi * 2 * nf * 128:(wi + 1) * 2 * nf * 128].rearrange("p (k n) -> p k n", k=2)
                jobs.append((d, src, wi))
        else:
            j = k[1]
            src = w_down[l, j * 256:(j + 1) * 256, :].rearrange("(k p) n -> p k n", p=128)
            jobs.append((dst.rearrange("p (k n) -> p k n", k=2), src, 0))
        for d, s_, wi in jobs:
            ch = st["cv"] % 8
            st["cv"] += 1
            S.add("pool", lambda e, d=d, s_=s_: e.dma_start(out=d, in_=s_),
                  writes=[("wbf", l, ui, wi), ("cvch", ch)], dma_chan=("cv", ch))

    tiles = [("p", t) for t in range(npt)] + ([("s", 0)] if with_sample else [])
    seq = [(ti, l, ui) for ti in range(len(tiles)) for l in range(2) for ui in range(NUNIT)]

    def emit_load(n):
        ti, l, ui = seq[n]
        s_ = n % NSLOT
        ne = 1024 if (UL[ui][0] == "gu" and UL[ui][1] == 5) else 2048
        S.add("sp", lambda e: e.dma_start(out=ring[:, s_, 0:ne], in_=wbf[l, ui][:, 0:ne]),
              reads=unit_keys(l, ui), writes=[("ring", s_)], dma_chan=("w", s_))

    def slot(n):
        assert n < st["ld"], (n, st["ld"])
        return n % NSLOT

    def done(n):
        m = n + NSLOT
        assert m == st["ld"], (m, st["ld"])
        if m < len(seq):
            emit_load(m)
        st["ld"] += 1

    def tile_src(ti):
        kind, t = tiles[ti]
        if kind == "p":
            return xp[t * 512:(t + 1) * 512, :], yp[t * 512:(t + 1) * 512, :], 4
        return xs, ys, 2

    def load_xres(ti):
        src, _, nch = tile_src(ti)
        for c in range(nch):
            S.add("pool", lambda e, c=c: e.dma_start(out=xres[:, c, :], in_=src[c * 128:(c + 1) * 128, :]),
                  writes=[("xres", c)], dma_chan=("xl", c))

    def load_xbf(ti):
        src, _, nch = tile_src(ti)
        for c in range(nch):
            S.add("pool", lambda e, c=c: e.dma_start(out=xbf[:, c, :], in_=src[c * 128:(c + 1) * 128, :]),
                  writes=[("xbf", c)], dma_chan=("xb", c))

    load_xbf(0)
    load_xres(0)
    for l in range(2):
        for ui in range(NUNIT):
            emit_conv(l, ui)
    for n in range(5):
        emit_load(n)

    def pdma(fn, reads=(), writes=()):
        ch = st["io"]
        st["io"] += 1
        return S.add("sp", fn, reads=list(reads), writes=list(writes), dma_chan=("pr", ch))

    pdma(lambda e: e.dma_start(out=idf[:], in_=ident), writes=["idf"])
    pdma(lambda e: e.dma_start(out=cst_sb[:], in_=cst), writes=["cst"])
    S.add("act", lambda e: e.copy(out=idb[:], in_=idf[:]), reads=["idf"], writes=["idb"])
    S.add("dve", lambda e: e.memset(epst[:], EPS), writes=["eps"])
    S.add("dve", lambda e: e.memset(ptail[:], 0.0), writes=["ptail0", "ptail1"])
    S.add("dve", lambda e: e.memset(ctail[:], 0.0), writes=["ctail0", "ctail1"])
    for l in range(2):
        pdma(lambda e, l=l: e.dma_start(out=pscale[:, l, :], in_=pool_scale[l].rearrange("(c p) -> p c", p=128)), writes=[("pscale", l)])
        for k in range(3):
            pdma(lambda e, l=l, k=k: e.dma_start(out=cw[:, l, :, k], in_=conv_w[l, k].rearrange("(c p) -> p c", p=128)), writes=[("cw", l, k)])

    stA = lscr[:, :, :].rearrange("p a (h j) -> p (a h) j", j=128)
    stB = hid[:, 0:8, :].rearrange("p a b -> p (a b)").bitcast(F32).rearrange("p (h j) -> p h j", j=128)
    catA = cat[:, 0:4, :].rearrange("p a (h j) -> p (a h) j", j=128)
    catB = cat[:, 4:8, :].rearrange("p a (h j) -> p (a h) j", j=128)
    KA = [("lscr", 0), ("lscr", 1)]
    KB = [("hid", f) for f in range(8)]
    pdma(lambda e: e.dma_start(out=stA, in_=w_spatial.rearrange("l h i j -> i (l h) j")), writes=KA)
    S.add("dve", lambda e: e.memset(stB, 0.0), writes=KB)
    wsub = w_spatial[:, :, 0:64, 0:64].rearrange("l h i j -> i (l h) j")
    pdma(lambda e: e.dma_start(out=stB[0:64, :, 0:64], in_=wsub), reads=KB, writes=["stB0"])
    pdma(lambda e: e.dma_start(out=stB[64:128, :, 64:128], in_=wsub), reads=KB, writes=["stB1"])
    S.add("act", lambda e: e.copy(out=catA, in_=stA), reads=KA, writes=[("cat", j) for j in range(4)])
    S.add("act", lambda e: e.copy(out=catB, in_=stB), reads=KB + ["stB0", "stB1"], writes=[("cat", j) for j in range(4, 8)] + KB)
    for var, src_bf, keys in ((0, catA, [("cat", j) for j in range(4)]), (1, catB, [("cat", j) for j in range(4, 8)])):
        for l in range(2):
            b = alloc1()
            for h in range(8):
                S.add("pe", lambda e, l=l, h=h, b=b, src_bf=src_bf: e.transpose(out=psb(b)[:, h, :], in_=src_bf[:, l * 8 + h, :],
                                                                               identity=idb[:]),
                      reads=keys + ["idb"], writes=[PS(b)])
            s0 = (l * 2 + var) * 8
            S.add("act", lambda e, b=b, s0=s0: e.copy(out=wsT[:, s0:s0 + 8, :], in_=psb(b)), reads=[PS(b)], writes=[("wsT", s0)])
            if var == 0:
                S.add("dve", lambda e, s0=s0: e.memset(wsT[64:128, s0:s0 + 8, 0:64], 0.0), writes=[("wsT", s0)])
    for l in range(2):
        bsrc = b_spatial[l].rearrange("(pr two) i -> two pr i", two=2)
        for hh in range(2):
            b0 = (l * 2) * 4
            pdma(lambda e, hh=hh, b0=b0, bsrc=bsrc: e.dma_start(out=bmat[hh * 64:(hh + 1) * 64, b0:b0 + 4, :],
                                                               in_=bsrc[hh].unsqueeze(0).broadcast_to([64, 4, 128])), writes=[("bmat", l, hh, 2)])
            b1 = (l * 2 + 1) * 4
            for half in range(2):
                pdma(lambda e, hh=hh, b1=b1, half=half, bsrc=bsrc: e.dma_start(
                    out=bmat[hh * 64:(hh + 1) * 64, b1:b1 + 4, half * 64:(half + 1) * 64],
                    in_=bsrc[hh][:, 0:64].unsqueeze(0).broadcast_to([64, 4, 64])), writes=[("bmat", l, hh, half)])
    stP = sscr[:, :, :].rearrange("p a (b j) -> p (a b) j", j=128)
    KP = [("sscr", 0), ("sscr", 1)]
    S.add("dve", lambda e: e.memset(stP[:, 0:4, :], 0.0), writes=KP)
    wpsrc = w_pool.rearrange("l (j gg) c e -> gg c (l j) e", gg=2)
    for gg in range(2):
        pdma(lambda e, gg=gg: e.dma_start(out=stP[gg * 64:(gg + 1) * 64, 0:4, gg * 64:(gg + 1) * 64], in_=wpsrc[gg]), reads=KP, writes=["stP%d" % gg])
    S.add("act", lambda e: e.copy(out=wpbd[:], in_=stP[:, 0:4, :]), reads=KP + ["stP0", "stP1"], writes=["wpbd"] + KP)

    def load_mix(l):
        S.add("sp", lambda e: e.dma_start(out=mixg[:], in_=ln_mix_g[l:l + 1, :].broadcast_to([128, D])),
              writes=["mixg"], dma_chan="pmg")
        S.add("sp", lambda e: e.dma_start(out=mixb[:], in_=ln_mix_b[l:l + 1, :].broadcast_to([128, D])),
              writes=["mixb"], dma_chan="pmb")

    def load_ffn(l):
        S.add("sp", lambda e: e.dma_start(out=ffng[:], in_=ln_ffn_g[l:l + 1, :].broadcast_to([128, D])),
              writes=["ffng"], dma_chan="pfg")
        S.add("sp", lambda e: e.dma_start(out=ffnb[:], in_=ln_ffn_b[l:l + 1, :].broadcast_to([128, D])),
              writes=["ffnb"], dma_chan="pfb")

    def load_vn(l):
        S.add("sp", lambda e: e.dma_start(out=vng[:], in_=v_norm_g[l:l + 1, :].broadcast_to([128, 512])),
              writes=["vng"], dma_chan="pvg")
        S.add("sp", lambda e: e.dma_start(out=vnb[:], in_=v_norm_b[l:l + 1, :].broadcast_to([128, 512])),
              writes=["vnb"], dma_chan="pvb")

    load_vn(0)
    for n in range(5, NSLOT):
        emit_load(n)
    st["ld"] = NSLOT
    load_mix(0)
    load_ffn(0)

    def transpose_chunk(c):
        b = alloc1()
        for k in range(8):
            S.add("pe", lambda e, k=k: e.transpose(out=psb(b)[:, k, :], in_=xbf[:, c, k * 128:(k + 1) * 128], identity=idb[:]),
                  reads=[("xbf", c), "idb"], writes=[PS(b)])
        S.add("act", lambda e: e.copy(out=xT[:, :, c * 128:(c + 1) * 128], in_=psb(b)), reads=[PS(b)], writes=[("xT", c)])

    def transposes(nch):
        for c in range(nch):
            transpose_chunk(c)

    def tile_layer(ti, l, base):
        kind, t = tiles[ti]
        sample = kind == "s"
        nch = 2 if sample else 4
        T = nch * 128
        nseg, seglen = (4, 64) if sample else (1, 512)
        var = 1 if sample else 0
        last_prompt = (kind == "p" and t == npt - 1)
        first_prompt = (kind == "p" and t == 0)
        XT_ALL = [("xT", c) for c in range(nch)]
        AE = "dve" if ti == 0 else "pool"
        LP, LC = nseg * (15 + seglen), nseg * (2 + seglen)

        def R(n):
            return ring[:, slot(base + n), :]

        def RK(n):
            return ("ring", slot(base + n))

        def xb_valid(j):
            return xbuf[:, j, 0:LP].rearrange("p (s q) -> p s q", s=nseg)[:, :, 15:]

        def z_valid(j, off=2):
            return zbuf[:, j, 0:LC].rearrange("p (s q) -> p s q", s=nseg)[:, :, off:off + seglen]

        def seg(ap2d):
            return ap2d.rearrange("p (s q) -> p s q", s=nseg)

        w0 = R(U_V0).rearrange("p (k n) -> p k n", k=4)
        w1 = R(U_V1).rearrange("p (k n) -> p k n", k=4)
        pend = st.pop("pendT", [])
        vbanks = []
        for c in range(nch):
            if c in pend:
                transpose_chunk(c)
            b = alloc1()
            vbanks.append(b)
            for kc in range(8):
                w = (w0 if kc < 4 else w1)[:, kc % 4, :]
                S.add("pe", lambda e, c=c, kc=kc, b=b, w=w: e.matmul(ps[:, b, :], lhsT=xT[:, kc, c * 128:(c + 1) * 128], rhs=w,
                                                                     start=(kc == 0), stop=(kc == 7)),
                      reads=[("xT", c), RK(U_V0 if kc < 4 else U_V1)], writes=[PS(b)])
            S.add("dve", lambda e, c=c, b=b: e.bn_stats(out=vst[:, c, :], in_=ps[:, b, :]), reads=[PS(b)], writes=[("vst", c)])
            S.add("dve", lambda e, c=c: e.bn_aggr(out=vmv[:, c, :], in_=vst[:, c, :]), reads=[("vst", c)], writes=[("vmv", c)])
        VMV = [("vmv", c) for c in range(nch)]
        S.add("act", lambda e: e.activation(out=vrs[:, 0:nch], in_=vmv[:, 0:nch, 1], func=AF.Sqrt, bias=epst[:, 0:1], scale=1.0),
              reads=VMV + ["eps"], writes=["vrs"])
        S.add("dve", lambda e: e.reciprocal(out=vrs[:, 0:nch], in_=vrs[:, 0:nch]), reads=["vrs"], writes=["vrs"])
        vo = lscr[:, 0, :].rearrange("p (a b) -> p a b", a=2)
        for c in range(nch):
            b = vbanks[c]
            S.add("dve", lambda e, c=c, b=b: e.scalar_tensor_tensor(out=vscr[:, c % 2, :], in0=ps[:, b, :], scalar=vmv[:, c, 0:1],
                                                                    in1=vng[:], op0=ALU.subtract, op1=ALU.mult),
                  reads=[PS(b), ("vmv", c), "vng"], writes=[("vscr", c % 2)])
            if sample:
                S.add("dve", lambda e, c=c: e.scalar_tensor_tensor(out=vo[:, c, :], in0=vscr[:, c % 2, :], scalar=vrs[:, c:c + 1],
                                                                   in1=vnb[:], op0=ALU.mult, op1=ALU.add),
                      reads=[("vscr", c % 2), "vrs", "vnb"], writes=[("lscr", 0)])
                S.add("act", lambda e, c=c: e.copy(out=vn[:, c, :], in_=vo[:, c, :]), reads=[("lscr", 0)], writes=[("vn", c)])
            else:
                S.add("dve", lambda e, c=c: e.scalar_tensor_tensor(out=vn[:, c, :], in0=vscr[:, c % 2, :], scalar=vrs[:, c:c + 1],
                                                                   in1=vnb[:], op0=ALU.mult, op1=ALU.add),
                      reads=[("vscr", c % 2), "vrs", "vnb"], writes=[("vn", c)])
        if sample:
            vo = lscr[:, 0, :].rearrange("p (a b) -> p a b", a=2)
            S.add("pool", lambda e, vo=vo: e.dma_start(out=o_v[l].rearrange("(c p) d -> p c d", p=128), in_=vo),
                  reads=[("lscr", 0)], dma_chan="ov")
        done(base + U_V0)
        done(base + U_V1)
        nxt_l = 1 - l
        if not (ti == len(tiles) - 1 and l == 1):
            load_vn(nxt_l)

        if sample:
            S.add("pool", lambda e: e.dma_start(out=stin[0:60, :], in_=spool[l]), writes=["stin"], dma_chan="si1")
            S.add("pool", lambda e: e.dma_start(out=stin2[0:8, :], in_=sconv[l]), writes=["stin2"], dma_chan="si2")
            for j in range(2):
                b = alloc1()
                S.add("pe", lambda e, j=j, b=b: e.matmul(ps[:, b, 0:60], lhsT=stin[0:60, j * 128:(j + 1) * 128], rhs=idf[0:60, 0:60],
                                                         start=True, stop=True),
                      reads=["stin", "idf"], writes=[PS(b)])
                S.add("act", lambda e, j=j, b=b: e.copy(out=xbuf[:, j, 0:LP].rearrange("p (s q) -> p s q", s=4)[:, :, 0:15],
                                                        in_=ps[:, b, 0:60].rearrange("p (s q) -> p s q", s=4)),
                      reads=[PS(b)], writes=[("xbuf", j)])
                b = alloc1()
                S.add("pe", lambda e, j=j, b=b: e.matmul(ps[:, b, 0:8], lhsT=stin2[0:8, j * 128:(j + 1) * 128], rhs=idf[0:8, 0:8],
                                                         start=True, stop=True),
                      reads=["stin2", "idf"], writes=[PS(b)])
                S.add("act", lambda e, j=j, b=b: e.copy(out=zbuf[:, j, 0:LC].rearrange("p (s q) -> p s q", s=4)[:, :, 0:2],
                                                        in_=ps[:, b, 0:8].rearrange("p (s q) -> p s q", s=4)),
                      reads=[PS(b)], writes=[("zbuf", j)])
        else:
            S.add("act", lambda e: e.copy(out=xbuf[:, :, 0:15], in_=ptail[:, l, :, :]), reads=["ptail%d" % l],
                  writes=[("xbuf", 0), ("xbuf", 1)])
            S.add("act", lambda e: e.copy(out=zbuf[:, :, 0:2], in_=ctail[:, l, :, :]), reads=["ctail%d" % l],
                  writes=[("zbuf", 0), ("zbuf", 1)])

        def fm_group(un, j, rhs_all=True):
            w = R(un).rearrange("p (k n) -> p k n", k=8)
            b = alloc1()
            for kc in range(8):
                S.add("pe", lambda e, kc=kc, b=b, w=w: e.matmul(ps[:, b, 0:T], lhsT=w[:, kc, j * 128:(j + 1) * 128], rhs=xT[:, kc, 0:T],
                                                                start=(kc == 0), stop=(kc == 7)),
                      reads=XT_ALL + [RK(un)], writes=[PS(b)])
            return b

        for j in range(2):
            b = fm_group(U_XB, j)
            S.add("act", lambda e, j=j, b=b: e.copy(out=xb_valid(j), in_=seg(ps[:, b, 0:T])), reads=[PS(b)], writes=[("xbuf", j)])
        if not sample:
            S.add("act", lambda e: e.copy(out=ptail[:, l, :, :], in_=xbuf[:, :, T:T + 15]), reads=[("xbuf", 0), ("xbuf", 1)],
                  writes=["ptail%d" % l])
        for j in range(2):
            b = fm_group(U_GC, j)
            S.add("act", lambda e, j=j, b=b: e.copy(out=z_valid(j), in_=seg(ps[:, b, 0:T])), reads=[PS(b)], writes=[("zbuf", j)])
        for j in range(2):
            b = fm_group(U_H, j)
            S.add("dve", lambda e, j=j, b=b: e.tensor_tensor(out=z_valid(j), in0=z_valid(j), in1=seg(ps[:, b, 0:T]), op=ALU.mult),
                  reads=[PS(b), ("zbuf", j)], writes=[("zbuf", j)])
        if not sample:
            S.add("act", lambda e: e.copy(out=ctail[:, l, :, :], in_=zbuf[:, :, T:T + 2]), reads=[("zbuf", 0), ("zbuf", 1)],
                  writes=["ctail%d" % l])

        st_chunks = list(range(nch)) if sample else ([nch - 1] if last_prompt else [])
        wxb = R(U_XB).rearrange("p (k n) -> p k n", k=8)
        wgc = R(U_GC).rearrange("p (k n) -> p k n", k=8)
        wh = R(U_H).rearrange("p (k n) -> p k n", k=8)
        for c in st_chunks:
            b = alloc1()
            for kc in range(8):
                S.add("pe", lambda e, c=c, kc=kc, b=b: e.matmul(ps[:, b, 0:256], lhsT=xT[:, kc, c * 128:(c + 1) * 128], rhs=wxb[:, kc, :],
                                                                start=(kc == 0), stop=(kc == 7)),
                      reads=[("xT", c), RK(U_XB)], writes=[PS(b)])
            S.add("act", lambda e, c=c, b=b: e.copy(out=sttm[:, 0, :], in_=ps[:, b, 0:256]), reads=[PS(b)], writes=[("sttm", 0)])
            b2 = alloc1()
            for wi, ww in enumerate((wgc, wh)):
                for kc in range(8):
                    S.add("pe", lambda e, c=c, kc=kc, b2=b2, wi=wi, ww=ww: e.matmul(
                        ps[:, b2, wi * 256:(wi + 1) * 256], lhsT=xT[:, kc, c * 128:(c + 1) * 128], rhs=ww[:, kc, :],
                        start=(kc == 0), stop=(kc == 7)),
                        reads=[("xT", c), RK(U_GC if wi == 0 else U_H)], writes=[PS(b2)])
            S.add("act", lambda e, c=c, b2=b2: e.copy(out=sttm2[:, 0, :], in_=ps[:, b2, 0:256]), reads=[PS(b2)],
                  writes=[("sttm2", 0)])
            S.add("dve", lambda e, c=c, b2=b2: e.tensor_tensor(out=sttm2[:, 0, :], in0=sttm2[:, 0, :], in1=ps[:, b2, 256:512],
                                                               op=ALU.mult),
                  reads=[PS(b2), ("sttm2", 0)], writes=[("sttm2", 0)])
            if sample:
                for hf in range(2):
                    q = 2 * c + hf
                    S.add("pool", lambda e, c=c, hf=hf, q=q: e.dma_start(out=o_pool_s[l, q * 15:(q + 1) * 15, :],
                                                                         in_=sttm[hf * 64 + 49:hf * 64 + 64, 0, :]),
                          reads=[("sttm", 0)], dma_chan=("so", 0))
                    S.add("pool", lambda e, c=c, hf=hf, q=q: e.dma_start(out=o_conv_s[l, q * 2:(q + 1) * 2, :],
                                                                         in_=sttm2[hf * 64 + 62:hf * 64 + 64, 0, :]),
                          reads=[("sttm2", 0)], dma_chan=("so", 1))
            else:
                S.add("pool", lambda e, c=c: e.dma_start(out=o_pool_p[l], in_=sttm[113:128, 0, :]),
                      reads=[("sttm", 0)], dma_chan=("so", 2))
                S.add("pool", lambda e, c=c: e.dma_start(out=o_conv_p[l], in_=sttm2[126:128, 0, :]),
                      reads=[("sttm2", 0)], dma_chan=("so", 3))
        done(base + U_XB)
        done(base + U_GC)
        done(base + U_H)

        rw = cst_sb[:, 0:2]
        msk = cst_sb[:, 2:4]
        for j in range(2):
            X = xbuf[:, j, :]
            A = pscr[:, 0, :]
            B = pscr[:, 1, :]
            XK, AK, BK = ("xbuf", j), ("pscr", 0), ("pscr", 1)
            S.add(AE, lambda e, X=X, A=A: e.tensor_tensor(out=A[:, 1:LP], in0=X[:, 1:LP], in1=X[:, 0:LP - 1], op=ALU.add),
                  reads=[XK], writes=[AK])
            if j == 0:
                S.add("dve", lambda e, A=A, B=B: e.scalar_tensor_tensor(out=B[:, 3:LP], in0=A[:, 1:LP - 2], scalar=msk[:, 0:1],
                                                                        in1=A[:, 3:LP], op0=ALU.mult, op1=ALU.add),
                      reads=[AK, "cst"], writes=[BK])
            else:
                S.add(AE, lambda e, A=A, B=B: e.tensor_tensor(out=B[:, 3:LP], in0=A[:, 3:LP], in1=A[:, 1:LP - 2], op=ALU.add),
                      reads=[AK], writes=[BK])
                S.add(AE, lambda e, A=A, B=B: e.tensor_tensor(out=A[:, 7:LP], in0=B[:, 7:LP], in1=B[:, 3:LP - 4], op=ALU.add),
                      reads=[BK], writes=[AK])
                S.add("dve", lambda e, A=A, B=B: e.scalar_tensor_tensor(out=B[:, 15:LP], in0=A[:, 7:LP - 8], scalar=msk[:, 1:2],
                                                                        in1=A[:, 15:LP], op0=ALU.mult, op1=ALU.add),
                      reads=[AK, "cst"], writes=[BK])
            Bv = B[:, 0:LP].rearrange("p (s q) -> p s q", s=nseg)[:, :, 15:]
            S.add("dve", lambda e, j=j, Bv=Bv: e.scalar_tensor_tensor(out=seg(diff[:, j, 0:T]), in0=Bv, scalar=rw[:, j:j + 1],
                                                                      in1=xb_valid(j), op0=ALU.mult, op1=ALU.subtract),
                  reads=[BK, XK, "cst"], writes=[("diff", j)])
            if first_prompt:
                rc = cst_sb[:, 4 + 15 * j:4 + 15 * (j + 1)]
                S.add("dve", lambda e, A=A, B=B, rc=rc: e.tensor_tensor(out=A[:, 0:15], in0=B[:, 15:30], in1=rc, op=ALU.mult),
                      reads=[BK, "cst"], writes=[AK])
                S.add("dve", lambda e, j=j, A=A, X=X: e.tensor_tensor(out=diff[:, j, 0:15], in0=A[:, 0:15], in1=X[:, 15:30], op=ALU.subtract),
                      reads=[AK, XK], writes=[("diff", j)])
        for j in range(2):
            ZK, YK = ("zbuf", j), ("ybuf", j)
            yv = seg(ybuf[:, j, 0:T])
            S.add("act", lambda e, j=j, yv=yv: e.activation(out=yv, in_=z_valid(j, 0), func=AF.Copy, scale=cw[:, l, j, 0:1]),
                  reads=[ZK, ("cw", l, 0), ("cw", l, 1), ("cw", l, 2)], writes=[YK])
            for k in (1, 2):
                tv = seg(ctmp[:, k - 1, 0:T])
                S.add("act", lambda e, j=j, tv=tv, k=k: e.activation(out=tv, in_=z_valid(j, k), func=AF.Copy, scale=cw[:, l, j, k:k + 1]),
                      reads=[ZK, ("cw", l, 0), ("cw", l, 1), ("cw", l, 2)], writes=[("ctmp", k)])
                S.add(AE, lambda e, yv=yv, tv=tv: e.tensor_tensor(out=yv, in0=yv, in1=tv, op=ALU.add),
                      reads=[YK, ("ctmp", k)], writes=[YK])

        for j in range(4):
            bz = alloc1()
            for c in range(nch):
                for hh in range(2):
                    h = 2 * j + hh
                    si = (l * 2 + var) * 8 + h
                    S.add("pe", lambda e, c=c, hh=hh, h=h, si=si, bz=bz: e.matmul(
                        ps[hh * 64:(hh + 1) * 64, bz, c * 128:(c + 1) * 128], lhsT=vn[:, c, h * 64:(h + 1) * 64], rhs=wsT[:, si, :],
                        start=True, stop=True),
                        reads=[("vn", c), ("wsT", (si // 8) * 8)], writes=[PS(bz)])
            un = U_U01 if j < 2 else U_U23
            bu = fm_group(un, j % 2)
            bi = (l * 2 + var) * 4 + j
            S.add("dve", lambda e, j=j, bz=bz, bi=bi: e.tensor_tensor(
                out=sscr[:, j % 2, 0:T].rearrange("p (c i) -> p c i", c=nch),
                in0=ps[:, bz, 0:T].rearrange("p (c i) -> p c i", c=nch),
                in1=bmat[:, bi:bi + 1, :].broadcast_to([128, nch, 128]), op=ALU.add),
                reads=[PS(bz)] + [("bmat", l, a_, b_) for a_ in range(2) for b_ in range(3)], writes=[("sscr", j % 2)])
            S.add("dve", lambda e, j=j, bu=bu: e.tensor_tensor(out=cat[:, j, 0:T], in0=sscr[:, j % 2, 0:T], in1=ps[:, bu, 0:T], op=ALU.mult),
                  reads=[PS(bu), ("sscr", j % 2)], writes=[("cat", j)])
            if j == 1:
                done(base + U_U01)
        done(base + U_U23)

        for j in range(2):
            b = alloc1()
            S.add("pe", lambda e, j=j, b=b: e.matmul(ps[:, b, 0:T], lhsT=wpbd[:, l * 2 + j, :], rhs=diff[:, j, 0:T], start=True, stop=True),
                  reads=[("diff", j), "wpbd"], writes=[PS(b)])
            S.add("act", lambda e, j=j, b=b: e.activation(out=cat[:, 4 + j, 0:T], in_=ps[:, b, 0:T], func=AF.Identity,
                                                          scale=pscale[:, l, j:j + 1]),
                  reads=[PS(b), ("pscale", l)], writes=[("cat", 4 + j)])
        for j in range(2):
            b = fm_group(U_GB, j)
            S.add("dve", lambda e, j=j, b=b: e.tensor_tensor(out=cat[:, 6 + j, 0:T], in0=ybuf[:, j, 0:T], in1=ps[:, b, 0:T], op=ALU.mult),
                  reads=[PS(b), ("ybuf", j)], writes=[("cat", 6 + j)])
        done(base + U_GB)

        def tm_proj_ln(un0, nk, src, src_keys, gam, bet, gk, bk, make_bf, store_out):
            def wk(k):
                return R(un0 + k // 2).rearrange("p (k n) -> p k n", k=2)[:, k % 2, :]

            pbs = {}

            def stage_a(c):
                pb = pbs[c]
                XK = ("xres", c)
                S.add("dve", lambda e: e.scalar_tensor_tensor(out=xres[:, c, :], in0=xres[:, c, :], scalar=ALPHA,
                                                              in1=ps[:, pb:pb + 2, :].rearrange("p a b -> p (a b)"),
                                                              op0=ALU.mult, op1=ALU.add, accum_out=lsum[:, c:c + 1]),
                      reads=[PS(pb), PS(pb + 1), XK], writes=[XK, ("lsum", c)])
                S.add("act", lambda e: e.activation(out=xbf[:, c, :], in_=xres[:, c, :], func=AF.Square, accum_out=lsq[:, c:c + 1]),
                      reads=[XK], writes=[("xbf", c), ("lsq", c)])
                S.add("dve", lambda e: e.tensor_scalar_mul(out=lmean[:, c:c + 1], in0=lsum[:, c:c + 1], scalar1=1.0 / D),
                      reads=[("lsum", c)], writes=[("lmean", c)])
                S.add("dve", lambda e: e.scalar_tensor_tensor(out=lnb[:, c:c + 1], in0=lmean[:, c:c + 1], scalar=-1.0, in1=lmean[:, c:c + 1],
                                                              op0=ALU.mult, op1=ALU.mult),
                      reads=[("lmean", c)], writes=[("lnb", c)])
                S.add("dve", lambda e: e.tensor_scalar_add(out=lnb[:, c:c + 1], in0=lnb[:, c:c + 1], scalar1=EPS),
                      reads=[("lnb", c)], writes=[("lnb", c)])
                S.add("dve", lambda e: e.scalar_tensor_tensor(out=lscr[:, c % 2, :], in0=xres[:, c, :], scalar=lmean[:, c:c + 1], in1=gam[:],
                                                              op0=ALU.subtract, op1=ALU.mult),
                      reads=[XK, ("lmean", c), gk], writes=[("lscr", c % 2)])
                S.add("act", lambda e: e.activation(out=lrs[:, c:c + 1], in_=lsq[:, c:c + 1], func=AF.Sqrt, scale=1.0 / D, bias=lnb[:, c:c + 1]),
                      reads=[("lsq", c), ("lnb", c)], writes=[("lrs", c)])

            def stage_b(c):
                XK = ("xres", c)
                S.add("dve", lambda e: e.reciprocal(out=lrs[:, c:c + 1], in_=lrs[:, c:c + 1]), reads=[("lrs", c)], writes=[("lrs", c)])
                S.add("dve", lambda e: e.scalar_tensor_tensor(out=xres[:, c, :], in0=lscr[:, c % 2, :], scalar=lrs[:, c:c + 1], in1=bet[:],
                                                              op0=ALU.mult, op1=ALU.add),
                      reads=[("lscr", c % 2), ("lrs", c), bk], writes=[XK])
                if make_bf:
                    S.add("act", lambda e: e.copy(out=xbf[:, c, :], in_=xres[:, c, :]), reads=[XK], writes=[("xbf", c)])
                if store_out is not None:
                    S.add("pool", lambda e: e.dma_start(out=store_out[c * 128:(c + 1) * 128, :], in_=xres[:, c, :]),
                          reads=[XK], dma_chan=("ys", c))

            def mm(c, mid_at=None, mid_hook=None):
                pb = pbs[c] = alloc2()
                for k in range(nk):
                    if mid_at is not None and k == mid_at:
                        mid_hook()
                    w = wk(k)
                    for hf in range(2):
                        S.add("pe", lambda e, k=k, hf=hf, w=w: e.matmul(ps[:, pb + hf, :], lhsT=src[:, k, c * 128:(c + 1) * 128],
                                                                        rhs=w[:, hf * 512:(hf + 1) * 512], start=(k == 0), stop=(k == nk - 1)),
                              reads=[src_keys[k], RK(un0 + k // 2)], writes=[PS(pb), PS(pb + 1)])
                    if c == nch - 1 and k % 2 == 1:
                        done(base + un0 + k // 2)

            pend_out = [nch - 2, nch - 1] if make_bf else []
            if nch == 4 and nk > 8:
                for c in range(3):
                    mm(c); stage_a(c); stage_b(c)
                if make_bf:
                    transpose_chunk(0)
                    transpose_chunk(1)
                    mm(3, mid_at=16, mid_hook=lambda: transpose_chunk(2))
                    pend_out = [3]
                else:
                    mm(3)
                stage_a(3); stage_b(3)
            elif nch == 4:
                mm(0); stage_a(0)
                mm(1); stage_a(1); stage_b(0)
                mm(2); stage_b(1)
                mm(3)
                if make_bf:
                    transpose_chunk(0)
                    transpose_chunk(1)
                stage_a(2); stage_a(3); stage_b(2); stage_b(3)
            else:
                for c in range(nch):
                    mm(c)
                    stage_a(c)
                    stage_b(c)
            return pend_out

        CAT_ALL = [("cat", j) for j in range(8)]
        pend1 = tm_proj_ln(U_OUT0, 8, cat, CAT_ALL, mixg, mixb, "mixg", "mixb", True, None)
        if not (ti == len(tiles) - 1 and l == 1):
            load_mix(nxt_l)

        def gu_block(g, fi, nf, c0, c1):
            f = 4 * g + fi
            t0, t1 = c0 * 128, c1 * 128
            xk = [("xT", c) for c in range(c0, c1)]
            banks = []
            for wi in range(2):
                b = alloc1()
                banks.append(b)
                for kc in range(8):
                    w = R(U_GU0 + 4 * g + kc // 2)[:, wi * 2 * nf * 128:(wi + 1) * 2 * nf * 128].rearrange("p (k n) -> p k n", k=2)
                    S.add("pe", lambda e, kc=kc, b=b, w=w: e.matmul(ps[:, b, t0:t1], lhsT=w[:, kc % 2, fi * 128:(fi + 1) * 128],
                                                                    rhs=xT[:, kc, t0:t1], start=(kc == 0), stop=(kc == 7)),
                          reads=xk + [RK(U_GU0 + 4 * g + kc // 2)], writes=[PS(b)])
            S.add("act", lambda e: e.activation(out=sscr[:, f % 2, t0:t1], in_=ps[:, banks[0], t0:t1], func=AF.Silu),
                  reads=[PS(banks[0])], writes=[("sscr", f % 2)])
            S.add("dve", lambda e: e.tensor_tensor(out=hid[:, f, t0:t1], in0=sscr[:, f % 2, t0:t1], in1=ps[:, banks[1], t0:t1], op=ALU.mult),
                  reads=[PS(banks[1]), ("sscr", f % 2)], writes=[("hid", f)])

        for g in range(6):
            nf = 4 if g < 5 else 2
            if g == 0 and nch == 4:
                for fi in range(nf):
                    gu_block(g, fi, nf, 0, 2)
                for c in pend1:
                    transpose_chunk(c)
                for fi in range(nf):
                    gu_block(g, fi, nf, 2, 4)
            else:
                if g == 0:
                    for c in pend1:
                        transpose_chunk(c)
                for fi in range(nf):
                    gu_block(g, fi, nf, 0, nch)
            for j in range(4):
                done(base + U_GU0 + 4 * g + j)
            if g == 0 and l == 1 and ti + 1 < len(tiles):
                load_xbf(ti + 1)

        if l == 1 and ti + 1 < len(tiles):
            transposes(tile_src(ti + 1)[2])

        HID_ALL = [("hid", f) for f in range(NFC)]
        _, ydst, _ = tile_src(ti)
        pend2 = tm_proj_ln(U_DN0, NFC, hid, HID_ALL, ffng, ffnb, "ffng", "ffnb", l == 0, ydst if l == 1 else None)
        if not (ti == len(tiles) - 1 and l == 1):
            load_ffn(nxt_l)
        if l == 0:
            st["pendT"] = pend2
        elif ti + 1 < len(tiles):
            load_xres(ti + 1)

    transposes(tile_src(0)[2])
    for ti in range(len(tiles)):
        for l in range(2):
            tile_layer(ti, l, (ti * 2 + l) * NUNIT)
    with nc.allow_non_contiguous_dma(reason="tiny per-partition parameter loads"):
        S.emit(final_wait_eng="sp")
    return nc


def _consts():
    cst = np.zeros((128, 34), np.float32)
    p = np.arange(128)
    up = (p >= 64)
    win = np.zeros((128, 2), np.float32)
    win[:, 0] = np.where(up, 4, 2)
    win[:, 1] = np.where(up, 16, 8)
    cst[:, 0:2] = 1.0 / win
    cst[:, 2:4] = up[:, None].astype(np.float32)
    for j in range(2):
        for t in range(15):
            cst[:, 4 + 15 * j + t] = 1.0 / np.minimum(t + 1, win[:, j])
    return cst


_WNAMES = ["ln_mix_g", "ln_mix_b", "ln_ffn_g", "ln_ffn_b", "v_norm_g", "v_norm_b", "w_in", "w_spatial", "b_spatial",
           "w_pool", "pool_scale", "conv_w", "w_out", "w_gate", "w_up", "w_down"]


def make_in_maps(inputs, n_cores, npt):
    f = lambda a: np.ascontiguousarray(np.asarray(a, dtype=np.float32))
    shared = {k: f(inputs[k]) for k in _WNAMES}
    shared["ident"] = np.eye(128, dtype=np.float32)
    shared["cst"] = _consts()
    xp = f(inputs["x_prompt"]); xs = f(inputs["x_sample"])
    sp = f(inputs["state_pool"]); sc = f(inputs["state_conv"])
    maps = []
    for i in range(n_cores):
        m = dict(shared)
        m["xp"] = np.ascontiguousarray(xp[i, :npt * 512])
        m["xs"] = np.ascontiguousarray(xs[4 * i:4 * i + 4].reshape(256, D))
        m["spool"] = np.ascontiguousarray(sp[:, 4 * i:4 * i + 4].reshape(2, 60, 256))
        m["sconv"] = np.ascontiguousarray(sc[:, 4 * i:4 * i + 4].reshape(2, 8, 256))
        maps.append(m)
    return maps


def gather(results, n_cores, npt):
    y_p = np.stack([r["yp"] for r in results], 0)
    y_s = np.concatenate([r["ys"].reshape(4, 64, D) for r in results], 0)
    pp = np.stack([r["o_pool_p"] for r in results], 1)
    pss = np.concatenate([r["o_pool_s"].reshape(2, 4, 15, 256) for r in results], 1)
    cp = np.stack([r["o_conv_p"] for r in results], 1)
    cs = np.concatenate([r["o_conv_s"].reshape(2, 4, 2, 256) for r in results], 1)
    vs = np.concatenate([r["o_v"].reshape(2, 4, 64, 512) for r in results], 1)
    return tuple(np.ascontiguousarray(a.astype(np.float32)) for a in (y_p, y_s, pp, pss, cp, cs, vs))


def kernel(**inputs):
    nc = build(NPT_FULL)
    maps = make_in_maps(inputs, N_CORES, NPT_FULL)
    res = run_bass_kernel_spmd(nc, maps, core_ids=list(range(N_CORES)))
    return gather(res.results, N_CORES, NPT_FULL)
```

```python
import contextlib
import numpy as np
import concourse.bass as bass
import concourse.mybir as mybir
from concourse.bass_utils import run_bass_kernel_spmd

F32 = mybir.dt.float32
BF16 = mybir.dt.bfloat16
AF = mybir.ActivationFunctionType
ALU = mybir.AluOpType

D = 1024
DFF = 2816
NFC = 22
ALPHA = float(4 ** 0.25)
EPS = 1e-5
NSLOT = 15
NUNIT = 47
U_V0, U_V1, U_XB, U_GC, U_H, U_U01, U_U23, U_GB, U_OUT0, U_GU0, U_DN0 = 0, 1, 2, 3, 4, 5, 6, 7, 8, 12, 36
N_CORES = 8
NPT_FULL = 16


class _Op:
    __slots__ = ("eng", "fn", "deps", "is_dma", "chan", "sig", "idx", "waits", "vc")

    def __init__(self, eng, fn, is_dma, chan):
        self.eng = eng
        self.fn = fn
        self.deps = []
        self.is_dma = is_dma
        self.chan = chan
        self.sig = 0
        self.idx = 0


class Sched:
    ENGS = ("pe", "act", "dve", "pool", "sp")

    def __init__(self, nc):
        self.nc = nc
        self.ops = {e: [] for e in self.ENGS}
        self.res = {}
        self.chan_count = {}
        self.n = 0

    def add(self, eng, fn, reads=(), writes=(), dma_chan=None):
        op = _Op(eng, fn, dma_chan is not None, dma_chan)
        op.idx = self.n
        self.n += 1
        deps = {}
        res = self.res

        def dep(o, raw):
            if (not o.is_dma) and (not op.is_dma) and o.eng == op.eng and op.eng == "pe":
                return
            deps[o.idx] = o

        for k in reads:
            st = res.get(k)
            if st is None:
                st = res[k] = [{}, {}]
            for o in st[0].values():
                dep(o, True)
        for k in writes:
            st = res.get(k)
            if st is None:
                st = res[k] = [{}, {}]
            for o in st[1].values():
                dep(o, False)
            for o in st[0].values():
                dep(o, False)
        tok = ("dma", dma_chan) if op.is_dma else eng
        for k in reads:
            res[k][1][tok] = op
        for k in writes:
            st = res[k]
            if st[1]:
                st[0] = {tok: op}
                st[1] = {}
            else:
                st[0][tok] = op
        op.deps = list(deps.values())
        if op.is_dma:
            c = self.chan_count.get(dma_chan, 0) + 1
            self.chan_count[dma_chan] = c
            op.sig = c
        self.ops[eng].append(op)
        return op

    def _plan_waits(self):
        allops = sorted((op for e in self.ENGS for op in self.ops[e]), key=lambda o: o.idx)
        K = {e: {} for e in self.ENGS}
        nsig = {e: 0 for e in self.ENGS}
        for op in allops:
            Ke = K[op.eng]
            need = {}
            for d in op.deps:
                if d.is_dma:
                    key, val = ("c", d.chan), 16 * d.sig
                else:
                    key, val = ("e", d.eng), d.sig
                cur = need.get(key)
                if cur is None or cur[0] < val:
                    need[key] = (val, d)
            todo = []
            for key, (val, d) in sorted(need.items(), key=lambda kv: -kv[1][1].idx):
                if Ke.get(key, 0) >= val:
                    continue
                todo.append((key, val))
                for k2, v2 in d.vc.items():
                    if Ke.get(k2, 0) < v2:
                        Ke[k2] = v2
                if Ke.get(key, 0) < val:
                    Ke[key] = val
            op.waits = todo
            if op.is_dma:
                vc = dict(Ke)
                vc[("c", op.chan)] = 16 * op.sig
                op.vc = vc
            else:
                if op.sig > 0:
                    nsig[op.eng] = op.sig
                if op.sig > 0:
                    vc = dict(Ke)
                    if vc.get(("e", op.eng), 0) < op.sig:
                        vc[("e", op.eng)] = op.sig
                    op.vc = vc
                else:
                    op.vc = None

    def emit(self, final_wait_eng="sp"):
        nc = self.nc
        for e in self.ENGS:
            for op in self.ops[e]:
                for d in op.deps:
                    if not d.is_dma and d.sig == 0:
                        d.sig = -1
        for e in self.ENGS:
            c = 0
            for op in self.ops[e]:
                if not op.is_dma and op.sig == -1:
                    c += 1
                    op.sig = c
        self._plan_waits()
        chans = sorted(self.chan_count.keys(), key=str)
        with contextlib.ExitStack() as st:
            esem = {e: st.enter_context(nc.semaphore("s_" + e)) for e in self.ENGS}
            csem = {}
            for i, c in enumerate(chans):
                csem[c] = st.enter_context(nc.semaphore("c%d" % i))
            block = st.enter_context(nc.Block())
            engobj = {"pe": block.tensor, "act": block.scalar, "dve": block.vector,
                      "pool": block.gpsimd, "sp": block.sync}

            def semof(key):
                return csem[key[1]] if key[0] == "c" else esem[key[1]]

            def make(e):
                def body(eng):
                    for op in self.ops[e]:
                        todo = op.waits
                        for key, val in todo[:-1]:
                            eng.wait_ge(semof(key), val)
                        inst = op.fn(eng)
                        if todo:
                            inst._wait_ge(semof(todo[-1][0]), todo[-1][1])
                        if op.is_dma:
                            inst.then_inc(csem[op.chan], 16)
                        elif op.sig > 0:
                            inst.then_inc(esem[op.eng], 1)
                    if e == final_wait_eng:
                        for c in chans:
                            eng.wait_ge(csem[c], 16 * self.chan_count[c])
                return body

            for e in self.ENGS:
                if self.ops[e] or e == final_wait_eng:
                    engobj[e](make(e))


def build(npt, with_sample=True):
    nc = bass.Bass("TRN2", target_bir_lowering=False)
    NP = npt * 512

    def din(name, shape):
        return nc.dram_tensor(name, shape, F32, kind="ExternalInput").ap()

    def dout(name, shape):
        return nc.dram_tensor(name, shape, F32, kind="ExternalOutput").ap()

    xp = din("xp", [NP, D])
    xs = din("xs", [256, D])
    spool = din("spool", [2, 60, 256])
    sconv = din("sconv", [2, 8, 256])
    ln_mix_g = din("ln_mix_g", [2, D]); ln_mix_b = din("ln_mix_b", [2, D])
    ln_ffn_g = din("ln_ffn_g", [2, D]); ln_ffn_b = din("ln_ffn_b", [2, D])
    v_norm_g = din("v_norm_g", [2, 512]); v_norm_b = din("v_norm_b", [2, 512])
    w_in = din("w_in", [2, D, 2048])
    w_spatial = din("w_spatial", [2, 8, 128, 128])
    b_spatial = din("b_spatial", [2, 8, 128])
    w_pool = din("w_pool", [2, 4, 64, 64])
    pool_scale = din("pool_scale", [2, 256])
    conv_w = din("conv_w", [2, 3, 256])
    w_out = din("w_out", [2, D, D])
    w_gate = din("w_gate", [2, D, DFF]); w_up = din("w_up", [2, D, DFF])
    w_down = din("w_down", [2, DFF, D])
    ident = din("ident", [128, 128])
    cst = din("cst", [128, 34])

    yp = dout("yp", [NP, D])
    ys = dout("ys", [256, D])
    o_pool_p = dout("o_pool_p", [2, 15, 256])
    o_pool_s = dout("o_pool_s", [2, 60, 256])
    o_conv_p = dout("o_conv_p", [2, 2, 256])
    o_conv_s = dout("o_conv_s", [2, 8, 256])
    o_v = dout("o_v", [2, 256, 512])

    wbf = nc.dram_tensor("wbf", [2, NUNIT, 128, 2048], BF16).ap()

    sb = nc.alloc_sbuf_tensor
    ring = sb("ring", [128, NSLOT, 2048], BF16)
    xres = sb("xres", [128, 4, D], F32)
    xbf = sb("xbf", [128, 4, D], BF16)
    xT = sb("xT", [128, 8, 512], BF16)
    vscr = sb("vscr", [128, 2, 512], F32)
    vn = sb("vn", [128, 4, 512], BF16)
    xbuf = sb("xbuf", [128, 2, 528], F32)
    pscr = sb("pscr", [128, 2, 528], F32)
    zbuf = sb("zbuf", [128, 2, 516], F32)
    ybuf = sb("ybuf", [128, 2, 512], F32)
    diff = sb("diff", [128, 2, 512], BF16)
    cat = sb("cat", [128, 8, 512], BF16)
    hid = sb("hid", [128, NFC, 512], BF16)
    sscr = sb("sscr", [128, 2, 512], F32)
    lscr = sb("lscr", [128, 2, D], F32)
    mixg = sb("mixg", [128, D], F32); mixb = sb("mixb", [128, D], F32)
    ffng = sb("ffng", [128, D], F32); ffnb = sb("ffnb", [128, D], F32)
    vng = sb("vng", [128, 512], F32); vnb = sb("vnb", [128, 512], F32)
    wsT = sb("wsT", [128, 32, 128], BF16)
    wpbd = sb("wpbd", [128, 4, 128], BF16)
    bmat = sb("bmat", [128, 16, 128], F32)
    ctmp = sb("ctmp", [128, 2, 512], F32)
    idf = sb("idf", [128, 128], F32)
    idb = sb("idb", [128, 128], BF16)
    cst_sb = sb("cst_sb", [128, 34], F32)
    pscale = sb("pscale", [128, 2, 2], F32)
    cw = sb("cw", [128, 2, 2, 3], F32)
    epst = sb("epst", [128, 1], F32)
    ptail = sb("ptail", [128, 2, 2, 15], F32)
    ctail = sb("ctail", [128, 2, 2, 2], F32)
    stin = sb("stin", [64, 256], F32)
    stin2 = sb("stin2", [8, 256], F32)
    sttm = sb("sttm", [128, 1, 256], F32)
    sttm2 = sb("sttm2", [128, 1, 256], F32)
    vst = sb("vst", [128, 4, 6], F32); vmv = sb("vmv", [128, 4, 2], F32); vrs = sb("vrs", [128, 4], F32)
    lsum = sb("lsum", [128, 4], F32); lmean = sb("lmean", [128, 4], F32)
    lnb = sb("lnb", [128, 4], F32); lsq = sb("lsq", [128, 4], F32); lrs = sb("lrs", [128, 4], F32)
    ps = nc.alloc_psum_tensor("ps", [128, 8, 512], F32)

    S = Sched(nc)
    st = {"pb": 0, "cv": 0, "ld": 0, "io": 0}

    def alloc1():
        b = st["pb"] % 8
        st["pb"] += 1
        return b

    def alloc2():
        if st["pb"] % 2:
            st["pb"] += 1
        b = st["pb"] % 8
        st["pb"] += 2
        return b

    def PS(b):
        return ("ps", b)

    def psb(b):
        return ps[:, b, :].bitcast(BF16).rearrange("p (k n) -> p k n", k=8)

    UL = [("in", 512, 0, 4), ("in", 512, 4, 4)]
    for c0 in (1024, 1536, 1792, 0, 256, 1280):
        UL.append(("inF", c0))
    for j in range(4):
        UL.append(("out", j))
    for g in range(6):
        for j in range(4):
            UL.append(("gu", g, j))
    for j in range(11):
        UL.append(("down", j))
    assert len(UL) == NUNIT

    def unit_keys(l, ui):
        if UL[ui][0] == "gu":
            return [("wbf", l, ui, 0), ("wbf", l, ui, 1)]
        return [("wbf", l, ui, 0)]

    def emit_conv(l, ui):
        k = UL[ui]
        dst = wbf[l, ui]
        jobs = []
        if k[0] == "in":
            _, c0, kc0, nk = k
            src = w_in[l, kc0 * 128:(kc0 + nk) * 128, c0:c0 + 512].rearrange("(k p) n -> p k n", p=128)
            jobs.append((dst.rearrange("p (k n) -> p k n", k=nk), src, 0))
        elif k[0] == "inF":
            c0 = k[1]
            src = w_in[l, :, c0:c0 + 256].rearrange("(k p) n -> p k n", p=128)
            jobs.append((dst.rearrange("p (k n) -> p k n", k=8), src, 0))
        elif k[0] == "out":
            j = k[1]
            src = w_out[l, j * 256:(j + 1) * 256, :].rearrange("(k p) n -> p k n", p=128)
            jobs.append((dst.rearrange("p (k n) -> p k n", k=2), src, 0))
        elif k[0] == "gu":
            _, g, j = k
            nf = 4 if g < 5 else 2
            f0 = 4 * g
            for wi, wsrc in enumerate((w_gate, w_up)):
                src = wsrc[l, j * 256:(j + 1) * 256, f0 * 128:(f0 + nf) * 128].rearrange("(k p) n -> p k n", p=128)
                d = dst[:, wi * 2 * nf * 128:(wi + 1) * 2 * nf * 128].rearrange("p (k n) -> p k n", k=2)
                jobs.append((d, src, wi))
        else:
            j = k[1]
            src = w_down[l, j * 256:(j + 1) * 256, :].rearrange("(k p) n -> p k n", p=128)
            jobs.append((dst.rearrange("p (k n) -> p k n", k=2), src, 0))
        for d, s_, wi in jobs:
            ch = st["cv"] % 8
            st["cv"] += 1
            S.add("pool", lambda e, d=d, s_=s_: e.dma_start(out=d, in_=s_),
                  writes=[("wbf", l, ui, wi), ("cvch", ch)], dma_chan=("cv", ch))

    tiles = [("p", t) for t in range(npt)] + ([("s", 0)] if with_sample else [])
    seq = [(ti, l, ui) for ti in range(len(tiles)) for l in range(2) for ui in range(NUNIT)]

    def emit_load(n):
        ti, l, ui = seq[n]
        s_ = n % NSLOT
        ne = 1024 if (UL[ui][0] == "gu" and UL[ui][1] == 5) else 2048
        S.add("sp", lambda e: e.dma_start(out=ring[:, s_, 0:ne], in_=wbf[l, ui][:, 0:ne]),
              reads=unit_keys(l, ui), writes=[("ring", s_)], dma_chan=("w", s_))

    def slot(n):
        assert n < st["ld"], (n, st["ld"])
        return n % NSLOT

    def done(n):
        m = n + NSLOT
        assert m == st["ld"], (m, st["ld"])
        if m < len(seq):
            emit_load(m)
        st["ld"] += 1

    def tile_src(ti):
        kind, t = tiles[ti]
        if kind == "p":
            return xp[t * 512:(t + 1) * 512, :], yp[t * 512:(t + 1) * 512, :], 4
        return xs, ys, 2

    def load_xres(ti):
        src, _, nch = tile_src(ti)
        for c in range(nch):
            S.add("pool", lambda e, c=c: e.dma_start(out=xres[:, c, :], in_=src[c * 128:(c + 1) * 128, :]),
                  writes=[("xres", c)], dma_chan=("xl", c))

    def load_xbf(ti):
        src, _, nch = tile_src(ti)
        for c in range(nch):
            S.add("pool", lambda e, c=c: e.dma_start(out=xbf[:, c, :], in_=src[c * 128:(c + 1) * 128, :]),
                  writes=[("xbf", c)], dma_chan=("xb", c))

    load_xbf(0)
    load_xres(0)
    for l in range(2):
        for ui in range(NUNIT):
            emit_conv(l, ui)
    for n in range(5):
        emit_load(n)

    def pdma(fn, reads=(), writes=()):
        ch = st["io"]
        st["io"] += 1
        return S.add("sp", fn, reads=list(reads), writes=list(writes), dma_chan=("pr", ch))

    pdma(lambda e: e.dma_start(out=idf[:], in_=ident), writes=["idf"])
    pdma(lambda e: e.dma_start(out=cst_sb[:], in_=cst), writes=["cst"])
    S.add("act", lambda e: e.copy(out=idb[:], in_=idf[:]), reads=["idf"], writes=["idb"])
    S.add("dve", lambda e: e.memset(epst[:], EPS), writes=["eps"])
    S.add("dve", lambda e: e.memset(ptail[:], 0.0), writes=["ptail0", "ptail1"])
    S.add("dve", lambda e: e.memset(ctail[:], 0.0), writes=["ctail0", "ctail1"])
    for l in range(2):
        pdma(lambda e, l=l: e.dma_start(out=pscale[:, l, :], in_=pool_scale[l].rearrange("(c p) -> p c", p=128)), writes=[("pscale", l)])
        for k in range(3):
            pdma(lambda e, l=l, k=k: e.dma_start(out=cw[:, l, :, k], in_=conv_w[l, k].rearrange("(c p) -> p c", p=128)), writes=[("cw", l, k)])

    stA = lscr[:, :, :].rearrange("p a (h j) -> p (a h) j", j=128)
    stB = hid[:, 0:8, :].rearrange("p a b -> p (a b)").bitcast(F32).rearrange("p (h j) -> p h j", j=128)
    catA = cat[:, 0:4, :].rearrange("p a (h j) -> p (a h) j", j=128)
    catB = cat[:, 4:8, :].rearrange("p a (h j) -> p (a h) j", j=128)
    KA = [("lscr", 0), ("lscr", 1)]
    KB = [("hid", f) for f in range(8)]
    pdma(lambda e: e.dma_start(out=stA, in_=w_spatial.rearrange("l h i j -> i (l h) j")), writes=KA)
    S.add("dve", lambda e: e.memset(stB, 0.0), writes=KB)
    wsub = w_spatial[:, :, 0:64, 0:64].rearrange("l h i j -> i (l h) j")
    pdma(lambda e: e.dma_start(out=stB[0:64, :, 0:64], in_=wsub), reads=KB, writes=["stB0"])
    pdma(lambda e: e.dma_start(out=stB[64:128, :, 64:128], in_=wsub), reads=KB, writes=["stB1"])
    S.add("act", lambda e: e.copy(out=catA, in_=stA), reads=KA, writes=[("cat", j) for j in range(4)])
    S.add("act", lambda e: e.copy(out=catB, in_=stB), reads=KB + ["stB0", "stB1"], writes=[("cat", j) for j in range(4, 8)] + KB)
    for var, src_bf, keys in ((0, catA, [("cat", j) for j in range(4)]), (1, catB, [("cat", j) for j in range(4, 8)])):
        for l in range(2):
            b = alloc1()
            for h in range(8):
                S.add("pe", lambda e, l=l, h=h, b=b, src_bf=src_bf: e.transpose(out=psb(b)[:, h, :], in_=src_bf[:, l * 8 + h, :],
                                                                               identity=idb[:]),
                      reads=keys + ["idb"], writes=[PS(b)])
            s0 = (l * 2 + var) * 8
            S.add("act", lambda e, b=b, s0=s0: e.copy(out=wsT[:, s0:s0 + 8, :], in_=psb(b)), reads=[PS(b)], writes=[("wsT", s0)])
            if var == 0:
                S.add("dve", lambda e, s0=s0: e.memset(wsT[64:128, s0:s0 + 8, 0:64], 0.0), writes=[("wsT", s0)])
    for l in range(2):
        bsrc = b_spatial[l].rearrange("(pr two) i -> two pr i", two=2)
        for hh in range(2):
            b0 = (l * 2) * 4
            pdma(lambda e, hh=hh, b0=b0, bsrc=bsrc: e.dma_start(out=bmat[hh * 64:(hh + 1) * 64, b0:b0 + 4, :],
                                                               in_=bsrc[hh].unsqueeze(0).broadcast_to([64, 4, 128])), writes=[("bmat", l, hh, 2)])
            b1 = (l * 2 + 1) * 4
            for half in range(2):
                pdma(lambda e, hh=hh, b1=b1, half=half, bsrc=bsrc: e.dma_start(
                    out=bmat[hh * 64:(hh + 1) * 64, b1:b1 + 4, half * 64:(half + 1) * 64],
                    in_=bsrc[hh][:, 0:64].unsqueeze(0).broadcast_to([64, 4, 64])), writes=[("bmat", l, hh, half)])
    stP = sscr[:, :, :].rearrange("p a (b j) -> p (a b) j", j=128)
    KP = [("sscr", 0), ("sscr", 1)]
    S.add("dve", lambda e: e.memset(stP[:, 0:4, :], 0.0), writes=KP)
    wpsrc = w_pool.rearrange("l (j gg) c e -> gg c (l j) e", gg=2)
    for gg in range(2):
        pdma(lambda e, gg=gg: e.dma_start(out=stP[gg * 64:(gg + 1) * 64, 0:4, gg * 64:(gg + 1) * 64], in_=wpsrc[gg]), reads=KP, writes=["stP%d" % gg])
    S.add("act", lambda e: e.copy(out=wpbd[:], in_=stP[:, 0:4, :]), reads=KP + ["stP0", "stP1"], writes=["wpbd"] + KP)

    def load_mix(l):
        S.add("sp", lambda e: e.dma_start(out=mixg[:], in_=ln_mix_g[l:l + 1, :].broadcast_to([128, D])),
              writes=["mixg"], dma_chan="pmg")
        S.add("sp", lambda e: e.dma_start(out=mixb[:], in_=ln_mix_b[l:l + 1, :].broadcast_to([128, D])),
              writes=["mixb"], dma_chan="pmb")

    def load_ffn(l):
        S.add("sp", lambda e: e.dma_start(out=ffng[:], in_=ln_ffn_g[l:l + 1, :].broadcast_to([128, D])),
              writes=["ffng"], dma_chan="pfg")
        S.add("sp", lambda e: e.dma_start(out=ffnb[:], in_=ln_ffn_b[l:l + 1, :].broadcast_to([128, D])),
              writes=["ffnb"], dma_chan="pfb")

    def load_vn(l):
        S.add("sp", lambda e: e.dma_start(out=vng[:], in_=v_norm_g[l:l + 1, :].broadcast_to([128, 512])),
              writes=["vng"], dma_chan="pvg")
        S.add("sp", lambda e: e.dma_start(out=vnb[:], in_=v_norm_b[l:l + 1, :].broadcast_to([128, 512])),
              writes=["vnb"], dma_chan="pvb")

    load_vn(0)
    for n in range(5, NSLOT):
        emit_load(n)
    st["ld"] = NSLOT
    load_mix(0)
    load_ffn(0)

    def transpose_chunk(c):
        b = alloc1()
        for k in range(8):
            S.add("pe", lambda e, k=k: e.transpose(out=psb(b)[:, k, :], in_=xbf[:, c, k * 128:(k + 1) * 128], identity=idb[:]),
                  reads=[("xbf", c), "idb"], writes=[PS(b)])
        S.add("act", lambda e: e.copy(out=xT[:, :, c * 128:(c + 1) * 128], in_=psb(b)), reads=[PS(b)], writes=[("xT", c)])

    def transposes(nch):
        for c in range(nch):
            transpose_chunk(c)

    def tile_layer(ti, l, base):
        kind, t = tiles[ti]
        sample = kind == "s"
        nch = 2 if sample else 4
        T = nch * 128
        nseg, seglen = (4, 64) if sample else (1, 512)
        var = 1 if sample else 0
        last_prompt = (kind == "p" and t == npt - 1)
        first_prompt = (kind == "p" and t == 0)
        XT_ALL = [("xT", c) for c in range(nch)]
        AE = "dve" if ti == 0 else "pool"
        LP, LC = nseg * (15 + seglen), nseg * (2 + seglen)

        def R(n):
            return ring[:, slot(base + n), :]

        def RK(n):
            return ("ring", slot(base + n))

        def xb_valid(j):
            return xbuf[:, j, 0:LP].rearrange("p (s q) -> p s q", s=nseg)[:, :, 15:]

        def z_valid(j, off=2):
            return zbuf[:, j, 0:LC].rearrange("p (s q) -> p s q", s=nseg)[:, :, off:off + seglen]

        def seg(ap2d):
            return ap2d.rearrange("p (s q) -> p s q", s=nseg)

        if sample:
            S.add("sp", lambda e: e.dma_start(out=stin[0:60, :], in_=spool[l]), writes=["stin"], dma_chan="si1")
            S.add("sp", lambda e: e.dma_start(out=stin2[0:8, :], in_=sconv[l]), writes=["stin2"], dma_chan="si2")

        w0 = R(U_V0).rearrange("p (k n) -> p k n", k=4)
        w1 = R(U_V1).rearrange("p (k n) -> p k n", k=4)
        pend = st.pop("pendT", [])
        vbanks = []
        for c in range(nch):
            if c in pend:
                transpose_chunk(c)
            b = alloc1()
            vbanks.append(b)
            for kc in range(8):
                w = (w0 if kc < 4 else w1)[:, kc % 4, :]
                S.add("pe", lambda e, c=c, kc=kc, b=b, w=w: e.matmul(ps[:, b, :], lhsT=xT[:, kc, c * 128:(c + 1) * 128], rhs=w,
                                                                     start=(kc == 0), stop=(kc == 7)),
                      reads=[("xT", c), RK(U_V0 if kc < 4 else U_V1)], writes=[PS(b)])
            S.add("dve", lambda e, c=c, b=b: e.bn_stats(out=vst[:, c, :], in_=ps[:, b, :]), reads=[PS(b)], writes=[("vst", c)])
            S.add("dve", lambda e, c=c: e.bn_aggr(out=vmv[:, c, :], in_=vst[:, c, :]), reads=[("vst", c)], writes=[("vmv", c)])
        VMV = [("vmv", c) for c in range(nch)]
        S.add("act", lambda e: e.activation(out=vrs[:, 0:nch], in_=vmv[:, 0:nch, 1], func=AF.Sqrt, bias=epst[:, 0:1], scale=1.0),
              reads=VMV + ["eps"], writes=["vrs"])
        S.add("dve", lambda e: e.reciprocal(out=vrs[:, 0:nch], in_=vrs[:, 0:nch]), reads=["vrs"], writes=["vrs"])
        vo = lscr[:, 0, :].rearrange("p (a b) -> p a b", a=2)
        for c in range(nch):
            b = vbanks[c]
            S.add("dve", lambda e, c=c, b=b: e.scalar_tensor_tensor(out=vscr[:, c % 2, :], in0=ps[:, b, :], scalar=vmv[:, c, 0:1],
                                                                    in1=vng[:], op0=ALU.subtract, op1=ALU.mult),
                  reads=[PS(b), ("vmv", c), "vng"], writes=[("vscr", c % 2)])
            if sample:
                S.add("dve", lambda e, c=c: e.scalar_tensor_tensor(out=vo[:, c, :], in0=vscr[:, c % 2, :], scalar=vrs[:, c:c + 1],
                                                                   in1=vnb[:], op0=ALU.mult, op1=ALU.add),
                      reads=[("vscr", c % 2), "vrs", "vnb"], writes=[("lscr", 0)])
                S.add("act", lambda e, c=c: e.copy(out=vn[:, c, :], in_=vo[:, c, :]), reads=[("lscr", 0)], writes=[("vn", c)])
            else:
                S.add("dve", lambda e, c=c: e.scalar_tensor_tensor(out=vn[:, c, :], in0=vscr[:, c % 2, :], scalar=vrs[:, c:c + 1],
                                                                   in1=vnb[:], op0=ALU.mult, op1=ALU.add),
                      reads=[("vscr", c % 2), "vrs", "vnb"], writes=[("vn", c)])
        if sample:
            vo = lscr[:, 0, :].rearrange("p (a b) -> p a b", a=2)
            S.add("pool", lambda e, vo=vo: e.dma_start(out=o_v[l].rearrange("(c p) d -> p c d", p=128), in_=vo),
                  reads=[("lscr", 0)], dma_chan="ov")
        done(base + U_V0)
        done(base + U_V1)
        nxt_l = 1 - l
        if not (ti == len(tiles) - 1 and l == 1):
            load_vn(nxt_l)

        if sample:
            for j in range(2):
                b = alloc1()
                S.add("pe", lambda e, j=j, b=b: e.matmul(ps[:, b, 0:60], lhsT=stin[0:60, j * 128:(j + 1) * 128], rhs=idf[0:60, 0:60],
                                                         start=True, stop=True),
                      reads=["stin", "idf"], writes=[PS(b)])
                S.add("act", lambda e, j=j, b=b: e.copy(out=xbuf[:, j, 0:LP].rearrange("p (s q) -> p s q", s=4)[:, :, 0:15],
                                                        in_=ps[:, b, 0:60].rearrange("p (s q) -> p s q", s=4)),
                      reads=[PS(b)], writes=[("xbuf", j)])
                b = alloc1()
                S.add("pe", lambda e, j=j, b=b: e.matmul(ps[:, b, 0:8], lhsT=stin2[0:8, j * 128:(j + 1) * 128], rhs=idf[0:8, 0:8],
                                                         start=True, stop=True),
                      reads=["stin2", "idf"], writes=[PS(b)])
                S.add("act", lambda e, j=j, b=b: e.copy(out=zbuf[:, j, 0:LC].rearrange("p (s q) -> p s q", s=4)[:, :, 0:2],
                                                        in_=ps[:, b, 0:8].rearrange("p (s q) -> p s q", s=4)),
                      reads=[PS(b)], writes=[("zbuf", j)])
        else:
            S.add("act", lambda e: e.copy(out=xbuf[:, :, 0:15], in_=ptail[:, l, :, :]), reads=["ptail%d" % l],
                  writes=[("xbuf", 0), ("xbuf", 1)])
            S.add("act", lambda e: e.copy(out=zbuf[:, :, 0:2], in_=ctail[:, l, :, :]), reads=["ctail%d" % l],
                  writes=[("zbuf", 0), ("zbuf", 1)])

        def fm_group(un, j, rhs_all=True):
            w = R(un).rearrange("p (k n) -> p k n", k=8)
            b = alloc1()
            for kc in range(8):
                S.add("pe", lambda e, kc=kc, b=b, w=w: e.matmul(ps[:, b, 0:T], lhsT=w[:, kc, j * 128:(j + 1) * 128], rhs=xT[:, kc, 0:T],
                                                                start=(kc == 0), stop=(kc == 7)),
                      reads=XT_ALL + [RK(un)], writes=[PS(b)])
            return b

        for j in range(2):
            b = fm_group(U_XB, j)
            S.add("act", lambda e, j=j, b=b: e.copy(out=xb_valid(j), in_=seg(ps[:, b, 0:T])), reads=[PS(b)], writes=[("xbuf", j)])
        if not sample:
            S.add("act", lambda e: e.copy(out=ptail[:, l, :, :], in_=xbuf[:, :, T:T + 15]), reads=[("xbuf", 0), ("xbuf", 1)],
                  writes=["ptail%d" % l])
        for j in range(2):
            b = fm_group(U_GC, j)
            S.add("act", lambda e, j=j, b=b: e.copy(out=z_valid(j), in_=seg(ps[:, b, 0:T])), reads=[PS(b)], writes=[("zbuf", j)])
        for j in range(2):
            b = fm_group(U_H, j)
            S.add("dve", lambda e, j=j, b=b: e.tensor_tensor(out=z_valid(j), in0=z_valid(j), in1=seg(ps[:, b, 0:T]), op=ALU.mult),
                  reads=[PS(b), ("zbuf", j)], writes=[("zbuf", j)])
        if not sample:
            S.add("act", lambda e: e.copy(out=ctail[:, l, :, :], in_=zbuf[:, :, T:T + 2]), reads=[("zbuf", 0), ("zbuf", 1)],
                  writes=["ctail%d" % l])

        st_chunks = list(range(nch)) if sample else ([nch - 1] if last_prompt else [])
        wxb = R(U_XB).rearrange("p (k n) -> p k n", k=8)
        wgc = R(U_GC).rearrange("p (k n) -> p k n", k=8)
        wh = R(U_H).rearrange("p (k n) -> p k n", k=8)
        for c in st_chunks:
            b = alloc1()
            for kc in range(8):
                S.add("pe", lambda e, c=c, kc=kc, b=b: e.matmul(ps[:, b, 0:256], lhsT=xT[:, kc, c * 128:(c + 1) * 128], rhs=wxb[:, kc, :],
                                                                start=(kc == 0), stop=(kc == 7)),
                      reads=[("xT", c), RK(U_XB)], writes=[PS(b)])
            S.add("act", lambda e, c=c, b=b: e.copy(out=sttm[:, 0, :], in_=ps[:, b, 0:256]), reads=[PS(b)], writes=[("sttm", 0)])
            b2 = alloc1()
            for wi, ww in enumerate((wgc, wh)):
                for kc in range(8):
                    S.add("pe", lambda e, c=c, kc=kc, b2=b2, wi=wi, ww=ww: e.matmul(
                        ps[:, b2, wi * 256:(wi + 1) * 256], lhsT=xT[:, kc, c * 128:(c + 1) * 128], rhs=ww[:, kc, :],
                        start=(kc == 0), stop=(kc == 7)),
                        reads=[("xT", c), RK(U_GC if wi == 0 else U_H)], writes=[PS(b2)])
            S.add("act", lambda e, c=c, b2=b2: e.copy(out=sttm2[:, 0, :], in_=ps[:, b2, 0:256]), reads=[PS(b2)],
                  writes=[("sttm2", 0)])
            S.add("dve", lambda e, c=c, b2=b2: e.tensor_tensor(out=sttm2[:, 0, :], in0=sttm2[:, 0, :], in1=ps[:, b2, 256:512],
                                                               op=ALU.mult),
                  reads=[PS(b2), ("sttm2", 0)], writes=[("sttm2", 0)])
            if sample:
                for hf in range(2):
                    q = 2 * c + hf
                    S.add("pool", lambda e, c=c, hf=hf, q=q: e.dma_start(out=o_pool_s[l, q * 15:(q + 1) * 15, :],
                                                                         in_=sttm[hf * 64 + 49:hf * 64 + 64, 0, :]),
                          reads=[("sttm", 0)], dma_chan=("so", 0))
                    S.add("pool", lambda e, c=c, hf=hf, q=q: e.dma_start(out=o_conv_s[l, q * 2:(q + 1) * 2, :],
                                                                         in_=sttm2[hf * 64 + 62:hf * 64 + 64, 0, :]),
                          reads=[("sttm2", 0)], dma_chan=("so", 1))
            else:
                S.add("pool", lambda e, c=c: e.dma_start(out=o_pool_p[l], in_=sttm[113:128, 0, :]),
                      reads=[("sttm", 0)], dma_chan=("so", 2))
                S.add("pool", lambda e, c=c: e.dma_start(out=o_conv_p[l], in_=sttm2[126:128, 0, :]),
                      reads=[("sttm2", 0)], dma_chan=("so", 3))
        done(base + U_XB)
        done(base + U_GC)
        done(base + U_H)

        rw = cst_sb[:, 0:2]
        msk = cst_sb[:, 2:4]
        for j in range(2):
            X = xbuf[:, j, :]
            A = pscr[:, 0, :]
            B = pscr[:, 1, :]
            XK, AK, BK = ("xbuf", j), ("pscr", 0), ("pscr", 1)
            S.add(AE, lambda e, X=X, A=A: e.tensor_tensor(out=A[:, 1:LP], in0=X[:, 1:LP], in1=X[:, 0:LP - 1], op=ALU.add),
                  reads=[XK], writes=[AK])
            if j == 0:
                S.add("dve", lambda e, A=A, B=B: e.scalar_tensor_tensor(out=B[:, 3:LP], in0=A[:, 1:LP - 2], scalar=msk[:, 0:1],
                                                                        in1=A[:, 3:LP], op0=ALU.mult, op1=ALU.add),
                      reads=[AK, "cst"], writes=[BK])
            else:
                S.add(AE, lambda e, A=A, B=B: e.tensor_tensor(out=B[:, 3:LP], in0=A[:, 3:LP], in1=A[:, 1:LP - 2], op=ALU.add),
                      reads=[AK], writes=[BK])
                S.add(AE, lambda e, A=A, B=B: e.tensor_tensor(out=A[:, 7:LP], in0=B[:, 7:LP], in1=B[:, 3:LP - 4], op=ALU.add),
                      reads=[BK], writes=[AK])
                S.add("dve", lambda e, A=A, B=B: e.scalar_tensor_tensor(out=B[:, 15:LP], in0=A[:, 7:LP - 8], scalar=msk[:, 1:2],
                                                                        in1=A[:, 15:LP], op0=ALU.mult, op1=ALU.add),
                      reads=[AK, "cst"], writes=[BK])
            Bv = B[:, 0:LP].rearrange("p (s q) -> p s q", s=nseg)[:, :, 15:]
            S.add("dve", lambda e, j=j, Bv=Bv: e.scalar_tensor_tensor(out=seg(diff[:, j, 0:T]), in0=Bv, scalar=rw[:, j:j + 1],
                                                                      in1=xb_valid(j), op0=ALU.mult, op1=ALU.subtract),
                  reads=[BK, XK, "cst"], writes=[("diff", j)])
            if first_prompt:
                rc = cst_sb[:, 4 + 15 * j:4 + 15 * (j + 1)]
                S.add("dve", lambda e, A=A, B=B, rc=rc: e.tensor_tensor(out=A[:, 0:15], in0=B[:, 15:30], in1=rc, op=ALU.mult),
                      reads=[BK, "cst"], writes=[AK])
                S.add("dve", lambda e, j=j, A=A, X=X: e.tensor_tensor(out=diff[:, j, 0:15], in0=A[:, 0:15], in1=X[:, 15:30], op=ALU.subtract),
                      reads=[AK, XK], writes=[("diff", j)])
        for j in range(2):
            ZK, YK = ("zbuf", j), ("ybuf", j)
            yv = seg(ybuf[:, j, 0:T])
            S.add("act", lambda e, j=j, yv=yv: e.activation(out=yv, in_=z_valid(j, 0), func=AF.Copy, scale=cw[:, l, j, 0:1]),
                  reads=[ZK, ("cw", l, 0), ("cw", l, 1), ("cw", l, 2)], writes=[YK])
            for k in (1, 2):
                tv = seg(ctmp[:, k - 1, 0:T])
                S.add("act", lambda e, j=j, tv=tv, k=k: e.activation(out=tv, in_=z_valid(j, k), func=AF.Copy, scale=cw[:, l, j, k:k + 1]),
                      reads=[ZK, ("cw", l, 0), ("cw", l, 1), ("cw", l, 2)], writes=[("ctmp", k)])
                S.add(AE, lambda e, yv=yv, tv=tv: e.tensor_tensor(out=yv, in0=yv, in1=tv, op=ALU.add),
                      reads=[YK, ("ctmp", k)], writes=[YK])

        for j in range(4):
            bz = alloc1()
            for c in range(nch):
                for hh in range(2):
                    h = 2 * j + hh
                    si = (l * 2 + var) * 8 + h
                    S.add("pe", lambda e, c=c, hh=hh, h=h, si=si, bz=bz: e.matmul(
                        ps[hh * 64:(hh + 1) * 64, bz, c * 128:(c + 1) * 128], lhsT=vn[:, c, h * 64:(h + 1) * 64], rhs=wsT[:, si, :],
                        start=True, stop=True),
                        reads=[("vn", c), ("wsT", (si // 8) * 8)], writes=[PS(bz)])
            un = U_U01 if j < 2 else U_U23
            bu = fm_group(un, j % 2)
            bi = (l * 2 + var) * 4 + j
            S.add("dve", lambda e, j=j, bz=bz, bi=bi: e.tensor_tensor(
                out=sscr[:, j % 2, 0:T].rearrange("p (c i) -> p c i", c=nch),
                in0=ps[:, bz, 0:T].rearrange("p (c i) -> p c i", c=nch),
                in1=bmat[:, bi:bi + 1, :].broadcast_to([128, nch, 128]), op=ALU.add),
                reads=[PS(bz)] + [("bmat", l, a_, b_) for a_ in range(2) for b_ in range(3)], writes=[("sscr", j % 2)])
            S.add("dve", lambda e, j=j, bu=bu: e.tensor_tensor(out=cat[:, j, 0:T], in0=sscr[:, j % 2, 0:T], in1=ps[:, bu, 0:T], op=ALU.mult),
                  reads=[PS(bu), ("sscr", j % 2)], writes=[("cat", j)])
            if j == 1:
                done(base + U_U01)
        done(base + U_U23)

        for j in range(2):
            b = alloc1()
            S.add("pe", lambda e, j=j, b=b: e.matmul(ps[:, b, 0:T], lhsT=wpbd[:, l * 2 + j, :], rhs=diff[:, j, 0:T], start=True, stop=True),
                  reads=[("diff", j), "wpbd"], writes=[PS(b)])
            S.add("act", lambda e, j=j, b=b: e.activation(out=cat[:, 4 + j, 0:T], in_=ps[:, b, 0:T], func=AF.Identity,
                                                          scale=pscale[:, l, j:j + 1]),
                  reads=[PS(b), ("pscale", l)], writes=[("cat", 4 + j)])
        for j in range(2):
            b = fm_group(U_GB, j)
            S.add("dve", lambda e, j=j, b=b: e.tensor_tensor(out=cat[:, 6 + j, 0:T], in0=ybuf[:, j, 0:T], in1=ps[:, b, 0:T], op=ALU.mult),
                  reads=[PS(b), ("ybuf", j)], writes=[("cat", 6 + j)])
        done(base + U_GB)

        def tm_proj_ln(un0, nk, src, src_keys, gam, bet, gk, bk, make_bf, store_out):
            def wk(k):
                return R(un0 + k // 2).rearrange("p (k n) -> p k n", k=2)[:, k % 2, :]

            pbs = {}

            def stage_a(c):
                pb = pbs[c]
                XK = ("xres", c)
                S.add("dve", lambda e: e.scalar_tensor_tensor(out=xres[:, c, :], in0=xres[:, c, :], scalar=ALPHA,
                                                              in1=ps[:, pb:pb + 2, :].rearrange("p a b -> p (a b)"),
                                                              op0=ALU.mult, op1=ALU.add, accum_out=lsum[:, c:c + 1]),
                      reads=[PS(pb), PS(pb + 1), XK], writes=[XK, ("lsum", c)])
                S.add("act", lambda e: e.activation(out=xbf[:, c, :], in_=xres[:, c, :], func=AF.Square, accum_out=lsq[:, c:c + 1]),
                      reads=[XK], writes=[("xbf", c), ("lsq", c)])
                S.add("dve", lambda e: e.tensor_scalar_mul(out=lmean[:, c:c + 1], in0=lsum[:, c:c + 1], scalar1=1.0 / D),
                      reads=[("lsum", c)], writes=[("lmean", c)])
                S.add("dve", lambda e: e.scalar_tensor_tensor(out=lnb[:, c:c + 1], in0=lmean[:, c:c + 1], scalar=-1.0, in1=lmean[:, c:c + 1],
                                                              op0=ALU.mult, op1=ALU.mult),
                      reads=[("lmean", c)], writes=[("lnb", c)])
                S.add("dve", lambda e: e.tensor_scalar_add(out=lnb[:, c:c + 1], in0=lnb[:, c:c + 1], scalar1=EPS),
                      reads=[("lnb", c)], writes=[("lnb", c)])
                S.add("dve", lambda e: e.scalar_tensor_tensor(out=lscr[:, c % 2, :], in0=xres[:, c, :], scalar=lmean[:, c:c + 1], in1=gam[:],
                                                              op0=ALU.subtract, op1=ALU.mult),
                      reads=[XK, ("lmean", c), gk], writes=[("lscr", c % 2)])
                S.add("act", lambda e: e.activation(out=lrs[:, c:c + 1], in_=lsq[:, c:c + 1], func=AF.Sqrt, scale=1.0 / D, bias=lnb[:, c:c + 1]),
                      reads=[("lsq", c), ("lnb", c)], writes=[("lrs", c)])

            def stage_b(c):
                XK = ("xres", c)
                S.add("dve", lambda e: e.reciprocal(out=lrs[:, c:c + 1], in_=lrs[:, c:c + 1]), reads=[("lrs", c)], writes=[("lrs", c)])
                S.add("dve", lambda e: e.scalar_tensor_tensor(out=xres[:, c, :], in0=lscr[:, c % 2, :], scalar=lrs[:, c:c + 1], in1=bet[:],
                                                              op0=ALU.mult, op1=ALU.add),
                      reads=[("lscr", c % 2), ("lrs", c), bk], writes=[XK])
                if make_bf:
                    S.add("act", lambda e: e.copy(out=xbf[:, c, :], in_=xres[:, c, :]), reads=[XK], writes=[("xbf", c)])
                if store_out is not None:
                    S.add("pool", lambda e: e.dma_start(out=store_out[c * 128:(c + 1) * 128, :], in_=xres[:, c, :]),
                          reads=[XK], dma_chan=("ys", c))

            def mm(c, mid_at=None, mid_hook=None):
                pb = pbs[c] = alloc2()
                for k in range(nk):
                    if mid_at is not None and k == mid_at:
                        mid_hook()
                    w = wk(k)
                    for hf in range(2):
                        S.add("pe", lambda e, k=k, hf=hf, w=w: e.matmul(ps[:, pb + hf, :], lhsT=src[:, k, c * 128:(c + 1) * 128],
                                                                        rhs=w[:, hf * 512:(hf + 1) * 512], start=(k == 0), stop=(k == nk - 1)),
                              reads=[src_keys[k], RK(un0 + k // 2)], writes=[PS(pb), PS(pb + 1)])
                    if c == nch - 1 and k % 2 == 1:
                        done(base + un0 + k // 2)

            pend_out = [nch - 2, nch - 1] if make_bf else []
            if nch == 4 and nk > 8:
                for c in range(3):
                    mm(c); stage_a(c); stage_b(c)
                if make_bf:
                    transpose_chunk(0)
                    transpose_chunk(1)
                    mm(3, mid_at=16, mid_hook=lambda: transpose_chunk(2))
                    pend_out = [3]
                else:
                    mm(3)
                stage_a(3); stage_b(3)
            elif nch == 4:
                mm(0); stage_a(0)
                mm(1); stage_a(1); stage_b(0)
                mm(2); stage_b(1)
                mm(3)
                if make_bf:
                    transpose_chunk(0)
                    transpose_chunk(1)
                stage_a(2); stage_a(3); stage_b(2); stage_b(3)
            else:
                for c in range(nch):
                    mm(c)
                    stage_a(c)
                    stage_b(c)
            return pend_out

        CAT_ALL = [("cat", j) for j in range(8)]
        pend1 = tm_proj_ln(U_OUT0, 8, cat, CAT_ALL, mixg, mixb, "mixg", "mixb", True, None)
        if not (ti == len(tiles) - 1 and l == 1):
            load_mix(nxt_l)

        def gu_block(g, fi, nf, c0, c1):
            f = 4 * g + fi
            t0, t1 = c0 * 128, c1 * 128
            xk = [("xT", c) for c in range(c0, c1)]
            banks = []
            for wi in range(2):
                b = alloc1()
                banks.append(b)
                for kc in range(8):
                    w = R(U_GU0 + 4 * g + kc // 2)[:, wi * 2 * nf * 128:(wi + 1) * 2 * nf * 128].rearrange("p (k n) -> p k n", k=2)
                    S.add("pe", lambda e, kc=kc, b=b, w=w: e.matmul(ps[:, b, t0:t1], lhsT=w[:, kc % 2, fi * 128:(fi + 1) * 128],
                                                                    rhs=xT[:, kc, t0:t1], start=(kc == 0), stop=(kc == 7)),
                          reads=xk + [RK(U_GU0 + 4 * g + kc // 2)], writes=[PS(b)])
            S.add("act", lambda e: e.activation(out=sscr[:, f % 2, t0:t1], in_=ps[:, banks[0], t0:t1], func=AF.Silu),
                  reads=[PS(banks[0])], writes=[("sscr", f % 2)])
            S.add("dve", lambda e: e.tensor_tensor(out=hid[:, f, t0:t1], in0=sscr[:, f % 2, t0:t1], in1=ps[:, banks[1], t0:t1], op=ALU.mult),
                  reads=[PS(banks[1]), ("sscr", f % 2)], writes=[("hid", f)])

        for g in range(6):
            nf = 4 if g < 5 else 2
            if g == 0 and nch == 4:
                for fi in range(nf):
                    gu_block(g, fi, nf, 0, 2)
                for c in pend1:
                    transpose_chunk(c)
                for fi in range(nf):
                    gu_block(g, fi, nf, 2, 4)
            else:
                if g == 0:
                    for c in pend1:
                        transpose_chunk(c)
                for fi in range(nf):
                    gu_block(g, fi, nf, 0, nch)
            for j in range(4):
                done(base + U_GU0 + 4 * g + j)
            if g == 0 and l == 1 and ti + 1 < len(tiles):
                load_xbf(ti + 1)

        if l == 1 and ti + 1 < len(tiles):
            transposes(tile_src(ti + 1)[2])

        HID_ALL = [("hid", f) for f in range(NFC)]
        _, ydst, _ = tile_src(ti)
        pend2 = tm_proj_ln(U_DN0, NFC, hid, HID_ALL, ffng, ffnb, "ffng", "ffnb", l == 0, ydst if l == 1 else None)
        if not (ti == len(tiles) - 1 and l == 1):
            load_ffn(nxt_l)
        if l == 0:
            st["pendT"] = pend2
        elif ti + 1 < len(tiles):
            load_xres(ti + 1)

    transposes(tile_src(0)[2])
    for ti in range(len(tiles)):
        for l in range(2):
            tile_layer(ti, l, (ti * 2 + l) * NUNIT)
    with nc.allow_non_contiguous_dma(reason="tiny per-partition parameter loads"):
        S.emit(final_wait_eng="sp")
    return nc


def _consts():
    cst = np.zeros((128, 34), np.float32)
    p = np.arange(128)
    up = (p >= 64)
    win = np.zeros((128, 2), np.float32)
    win[:, 0] = np.where(up, 4, 2)
    win[:, 1] = np.where(up, 16, 8)
    cst[:, 0:2] = 1.0 / win
    cst[:, 2:4] = up[:, None].astype(np.float32)
    for j in range(2):
        for t in range(15):
            cst[:, 4 + 15 * j + t] = 1.0 / np.minimum(t + 1, win[:, j])
    return cst


_WNAMES = ["ln_mix_g", "ln_mix_b", "ln_ffn_g", "ln_ffn_b", "v_norm_g", "v_norm_b", "w_in", "w_spatial", "b_spatial",
           "w_pool", "pool_scale", "conv_w", "w_out", "w_gate", "w_up", "w_down"]


def make_in_maps(inputs, n_cores, npt):
    f = lambda a: np.ascontiguousarray(np.asarray(a, dtype=np.float32))
    shared = {k: f(inputs[k]) for k in _WNAMES}
    shared["ident"] = np.eye(128, dtype=np.float32)
    shared["cst"] = _consts()
    xp = f(inputs["x_prompt"]); xs = f(inputs["x_sample"])
    sp = f(inputs["state_pool"]); sc = f(inputs["state_conv"])
    maps = []
    for i in range(n_cores):
        m = dict(shared)
        m["xp"] = np.ascontiguousarray(xp[i, :npt * 512])
        m["xs"] = np.ascontiguousarray(xs[4 * i:4 * i + 4].reshape(256, D))
        m["spool"] = np.ascontiguousarray(sp[:, 4 * i:4 * i + 4].reshape(2, 60, 256))
        m["sconv"] = np.ascontiguousarray(sc[:, 4 * i:4 * i + 4].reshape(2, 8, 256))
        maps.append(m)
    return maps


def gather(results, n_cores, npt):
    y_p = np.stack([r["yp"] for r in results], 0)
    y_s = np.concatenate([r["ys"].reshape(4, 64, D) for r in results], 0)
    pp = np.stack([r["o_pool_p"] for r in results], 1)
    pss = np.concatenate([r["o_pool_s"].reshape(2, 4, 15, 256) for r in results], 1)
    cp = np.stack([r["o_conv_p"] for r in results], 1)
    cs = np.concatenate([r["o_conv_s"].reshape(2, 4, 2, 256) for r in results], 1)
    vs = np.concatenate([r["o_v"].reshape(2, 4, 64, 512) for r in results], 1)
    return tuple(np.ascontiguousarray(a.astype(np.float32)) for a in (y_p, y_s, pp, pss, cp, cs, vs))


def kernel(**inputs):
    nc = build(NPT_FULL)
    maps = make_in_maps(inputs, N_CORES, NPT_FULL)
    res = run_bass_kernel_spmd(nc, maps, core_ids=list(range(N_CORES)))
    return gather(res.results, N_CORES, NPT_FULL)
```
